# Optimizing a Trainium2 kernel written in Bass

```python
import jax, jax.numpy as jnp
from jax import lax
import numpy as np

D_MODEL = 1024
BATCH = 32
SEQ = 2048
DEPTH = 1

SB_HEAD_DIM = 64
SB_WIDTH = D_MODEL // 2
SB_HEADS = SB_WIDTH // SB_HEAD_DIM
GLA_HEADS = 4
GLA_WIDTH = D_MODEL - SB_WIDTH
GLA_DV = GLA_WIDTH // GLA_HEADS
GLA_DK = GLA_DV // 2
GLA_KEY_WIDTH = GLA_HEADS * GLA_DK
GLA_GATE_RANK = 16
GLA_GATE_NORMALIZER = 16.0
GLA_CHUNK = 64
Q_BLOCK = 128
D_FF = 2816
CONV_WIDTH = 3
EPS = 1e-6

IN_SPLITS = (SB_WIDTH, SB_WIDTH, SB_WIDTH,
             GLA_KEY_WIDTH, GLA_KEY_WIDTH, GLA_WIDTH,
             GLA_GATE_RANK, GLA_WIDTH)
IN_COLS = sum(IN_SPLITS)

kernel_name = "hybrid_stickbreak_gla_convffn"


def rms_norm(x, g):
    xf = x.astype(jnp.float32)
    y = xf * lax.rsqrt(jnp.mean(xf * xf, axis=-1, keepdims=True) + EPS)
    return (y * g.astype(jnp.float32)).astype(x.dtype)


def head_rms_norm(o, g):
    B, H, S, d = o.shape
    of = o.astype(jnp.float32)
    of = of * lax.rsqrt(jnp.mean(of * of, axis=-1, keepdims=True) + EPS)
    of = jnp.transpose(of, (0, 2, 1, 3)).reshape(B, S, H * d)
    return (of * g.astype(jnp.float32)).astype(o.dtype)


def to_heads(t, n_heads):
    B, S, W = t.shape
    return jnp.transpose(t.reshape(B, S, n_heads, W // n_heads), (0, 2, 1, 3))


def stick_breaking_attention(q, k, v):
    B, H, S, d = q.shape
    scale = d ** -0.5
    outs = []
    for i in range(S // Q_BLOCK):
        q0 = i * Q_BLOCK
        kn = q0 + Q_BLOCK
        z = jnp.einsum('bhqd,bhkd->bhqk', q[:, :, q0:kn], k[:, :, :kn]).astype(jnp.float32) * scale
        t_idx = q0 + jnp.arange(Q_BLOCK)[:, None]
        s_idx = jnp.arange(kn)[None, :]
        strict = s_idx < t_idx
        log1m = jnp.where(strict, -jax.nn.softplus(z), 0.0)
        after = lax.cumsum(log1m, axis=3, reverse=True) - log1m
        w = jnp.where(strict, jnp.exp(jax.nn.log_sigmoid(z) + after), 0.0)
        outs.append(jnp.einsum('bhqk,bhkd->bhqd', w.astype(v.dtype), v[:, :, :kn]))
    return jnp.concatenate(outs, axis=2)


def gla_chunked(q, k, v, log_a):
    B, H, S, dk = q.shape
    dv = v.shape[-1]
    C = GLA_CHUNK
    N = S // C
    f32 = jnp.float32
    qf = q.astype(f32).reshape(B, H, N, C, dk) * (dk ** -0.5)
    kf = k.astype(f32).reshape(B, H, N, C, dk)
    vf = v.astype(f32).reshape(B, H, N, C, dv)
    b = jnp.cumsum(log_a.astype(f32).reshape(B, H, N, C, dk), axis=3)
    b_last = b[:, :, :, -1:, :]
    q_dec = qf * jnp.exp(b)
    k_inv = kf * jnp.exp(-b)
    k_end = kf * jnp.exp(b_last - b)
    causal = jnp.tril(jnp.ones((C, C), dtype=f32))
    attn = jnp.einsum('bhnik,bhnjk->bhnij', q_dec, k_inv) * causal
    o_intra = jnp.einsum('bhnij,bhnjv->bhniv', attn, vf)
    chunk_kv = jnp.einsum('bhnck,bhncv->bhnkv', k_end, vf)
    decay = jnp.exp(b_last[:, :, :, 0, :])

    def step(state, inp):
        kv_n, dec_n = inp
        return dec_n[..., None] * state + kv_n, state

    init = jnp.zeros((B, H, dk, dv), dtype=f32)
    _, prev = lax.scan(step, init, (jnp.moveaxis(chunk_kv, 2, 0), jnp.moveaxis(decay, 2, 0)))
    prev = jnp.moveaxis(prev, 0, 2)
    o_inter = jnp.einsum('bhnck,bhnkv->bhncv', q_dec, prev)
    return (o_intra + o_inter).reshape(B, H, S, dv).astype(v.dtype)


def causal_depthwise_conv(u, w, bias):
    S = u.shape[1]
    up = jnp.pad(u, ((0, 0), (CONV_WIDTH - 1, 0), (0, 0)))
    out = bias
    for tap in range(CONV_WIDTH):
        out = out + w[tap] * up[:, tap:tap + S]
    return out


def setup_inputs(seed: int = 0) -> dict:
    key = jax.random.key(seed)
    ks = jax.random.split(key, 16)
    nrm = lambda k, shape, s: jax.random.normal(k, shape, dtype=jnp.float32) * s
    gain = lambda k, shape: 1.0 + 0.02 * jax.random.normal(k, shape, dtype=jnp.float32)
    return {
        "x": nrm(ks[0], (BATCH, SEQ, D_MODEL), 1.0),
        "attn_norm_g": gain(ks[1], (DEPTH, D_MODEL)),
        "w_in": nrm(ks[2], (DEPTH, D_MODEL, IN_COLS), D_MODEL ** -0.5),
        "w_gate_up": nrm(ks[3], (DEPTH, GLA_GATE_RANK, GLA_KEY_WIDTH), GLA_GATE_RANK ** -0.5),
        "b_gate_up": nrm(ks[4], (DEPTH, GLA_KEY_WIDTH), 0.1),
        "sb_out_g": gain(ks[5], (DEPTH, SB_WIDTH)),
        "gla_out_g": gain(ks[6], (DEPTH, GLA_WIDTH)),
        "w_out": nrm(ks[7], (DEPTH, D_MODEL, D_MODEL), D_MODEL ** -0.5),
        "ffn_norm_g": gain(ks[8], (DEPTH, D_MODEL)),
        "w_ffn_up": nrm(ks[9], (DEPTH, D_MODEL, 2 * D_FF), D_MODEL ** -0.5),
        "conv_w": nrm(ks[10], (DEPTH, CONV_WIDTH, 2 * D_FF), CONV_WIDTH ** -0.5),
        "conv_b": nrm(ks[11], (DEPTH, 2 * D_FF), 0.01),
        "w_ffn_down": nrm(ks[12], (DEPTH, D_FF, D_MODEL), D_FF ** -0.5),
        "final_norm_g": gain(ks[13], (D_MODEL,)),
    }


def reference(x, attn_norm_g, w_in, w_gate_up, b_gate_up, sb_out_g, gla_out_g, w_out,
              ffn_norm_g, w_ffn_up, conv_w, conv_b, w_ffn_down, final_norm_g):
    offsets = list(np.cumsum(IN_SPLITS)[:-1])
    for l in range(DEPTH):
        h = rms_norm(x, attn_norm_g[l])
        proj = h @ w_in[l]
        sb_q, sb_k, sb_v, g_q, g_k, g_v, g_lr, g_og = jnp.split(proj, offsets, axis=-1)

        o_sb = stick_breaking_attention(to_heads(sb_q, SB_HEADS), to_heads(sb_k, SB_HEADS),
                                        to_heads(sb_v, SB_HEADS))
        o_sb = head_rms_norm(o_sb, sb_out_g[l])

        log_a = jax.nn.log_sigmoid((g_lr @ w_gate_up[l] + b_gate_up[l]).astype(jnp.float32)) / GLA_GATE_NORMALIZER
        o_gla = gla_chunked(to_heads(g_q, GLA_HEADS), to_heads(g_k, GLA_HEADS),
                            to_heads(g_v, GLA_HEADS), to_heads(log_a, GLA_HEADS))
        o_gla = head_rms_norm(o_gla, gla_out_g[l]) * jax.nn.silu(g_og)

        x = x + jnp.concatenate([o_sb, o_gla], axis=-1) @ w_out[l]

        h = rms_norm(x, ffn_norm_g[l])
        u = causal_depthwise_conv(h @ w_ffn_up[l], conv_w[l], conv_b[l])
        a, val = jnp.split(u, 2, axis=-1)
        x = x + (jax.nn.silu(a) * val) @ w_ffn_down[l]
    return rms_norm(x, final_norm_g)
```

```python
import numpy as np
from contextlib import ExitStack
import concourse.bass as bass
import concourse.mybir as mybir
from concourse.bass_utils import run_bass_kernel_spmd

F32 = mybir.dt.float32
BF = mybir.dt.bfloat16
AF = mybir.ActivationFunctionType
ALU = mybir.AluOpType

D = 1024
SEQ = 2048
NCORES = 8
DFF = 2816
EPS = 1e-6
NTT = SEQ // 128

ENGS = ('pe', 'act', 'dve', 'pool', 'sp')
EIDX = {e: i for i, e in enumerate(ENGS)}


class Sched:
    R = 4
    KD = 12

    def __init__(self, nc, es):
        self.nc = nc
        self.esem = {e: [es.enter_context(nc.semaphore(f"s_{e}{i}")) for i in range(self.R)]
                     for e in ENGS[:4]}
        self.dsem = {e: [es.enter_context(nc.semaphore(f"d_{e}{i}")) for i in range(self.KD)]
                     for e in ('sp', 'pool')}
        self.cnt = {e: 0 for e in ENGS}
        self.clock = {e: [0] * 5 for e in ENGS}
        self.snap = {e: [None] for e in ENGS}
        self.dcnt = {e: 0 for e in self.dsem}
        self.dseen = {e: {} for e in ENGS}
        self.lastw = {}
        self.readers = {}
        self.prog = {e: [] for e in ENGS}
        self.nwaits = 0
        self.nops = 0

    def _deps(self, eng, reads, writes):
        deps = set()
        for r in reads:
            ev = self.lastw.get(r)
            if ev is not None:
                deps.add(ev)
        for w in writes:
            ev = self.lastw.get(w)
            if ev is not None:
                deps.add(ev)
            for ev in self.readers.get(w, ()):
                deps.add(ev)
        best = {}
        for d in deps:
            if d[0] == 'c':
                _, e2, k = d
                if e2 == eng and eng == 'pe':
                    continue
                if self.clock[eng][EIDX[e2]] >= k:
                    continue
                if best.get(e2, 0) < k:
                    best[e2] = k
            else:
                _, pe_, idx, val = d
                if self.dseen[eng].get((pe_, idx), 0) >= val:
                    continue
                self._wait_dma(eng, pe_, idx, val)
        for e2, k in best.items():
            if self.clock[eng][EIDX[e2]] >= k:
                continue
            self._wait_eng(eng, e2, k)

    def _semval(self, k):
        r = k % self.R
        first = r if r != 0 else self.R
        return (k - first) // self.R + 1

    def _wait_eng(self, eng, e2, k):
        sem = self.esem[e2][k % self.R]
        self.prog[eng].append(('w', sem, self._semval(k)))
        self.nwaits += 1
        sn = self.snap[e2][k]
        c = self.clock[eng]
        for i in range(5):
            if sn[i] > c[i]:
                c[i] = sn[i]

    def _wait_dma(self, eng, pe_, idx, val):
        self.prog[eng].append(('w', self.dsem[pe_][idx], val))
        self.nwaits += 1
        self.dseen[eng][(pe_, idx)] = val

    def _record(self, ev, reads, writes):
        for w in writes:
            self.lastw[w] = ev
            self.readers[w] = []
        for r in reads:
            lst = self.readers.setdefault(r, [])
            if ev[0] == 'c':
                lst[:] = [x for x in lst if not (x[0] == 'c' and x[1] == ev[1])]
            lst.append(ev)

    def op(self, eng, fn, reads=(), writes=()):
        self._deps(eng, reads, writes)
        self.cnt[eng] += 1
        k = self.cnt[eng]
        sn = list(self.clock[eng])
        sn[EIDX[eng]] = k
        self.snap[eng].append(tuple(sn))
        self.prog[eng].append(('o', fn, self.esem[eng][k % self.R], 1))
        self.nops += 1
        self._record(('c', eng, k), reads, writes)

    def dma(self, eng, fn, reads=(), writes=()):
        self._deps(eng, reads, writes)
        d = self.dcnt[eng]
        self.dcnt[eng] += 1
        idx = d % self.KD
        prev = 16 * (d // self.KD)
        if prev > 0 and self.dseen[eng].get((eng, idx), 0) < prev:
            self._wait_dma(eng, eng, idx, prev)
        self.prog[eng].append(('o', fn, self.dsem[eng][idx], 16))
        self.nops += 1
        self._record(('d', eng, idx, prev + 16), reads, writes)

    def _final_vals(self):
        out = {}
        for pe_ in self.dsem:
            n = self.dcnt[pe_]
            for idx in range(min(n, self.KD)):
                out[(pe_, idx)] = 16 * ((n - idx + self.KD - 1) // self.KD)
        return out

    def flush(self):
        nc = self.nc
        fv = self._final_vals()
        for (pe_, idx), val in fv.items():
            if self.dseen['sp'].get((pe_, idx), 0) < val:
                self._wait_dma('sp', pe_, idx, val)
        prog = self.prog

        def mk(name):
            items = prog[name]

            def f(e):
                for it in items:
                    if it[0] == 'w':
                        e.wait_ge(it[1], it[2])
                    else:
                        it[1](e).then_inc(it[2], it[3])
            return f

        with nc.Block(no_gpsimd_drain=True) as block:
            block.tensor(mk('pe'))
            block.scalar(mk('act'))
            block.vector(mk('dve'))
            block.gpsimd(mk('pool'))
            block.sync(mk('sp'))
        self.prog = {e: [] for e in ENGS}
        full = [self.cnt[e] for e in ENGS]
        for e in ENGS:
            self.clock[e] = list(full)
            self.dseen[e] = dict(fv)
        self.lastw = {}
        self.readers = {}


def _consts():
    i = np.arange(128)
    a, b = i[:, None], i[None, :]
    same = (a // 64) == (b // 64)
    c = {}
    c["ident"] = np.eye(128, dtype=np.float32)
    c["triinc_neg"] = -(a >= b).astype(np.float32)
    c["mask_strict"] = (a < b).astype(np.float32)
    gc = -1.0 / 16.0
    c["gl1"] = (gc * ((a > b) & same)).astype(np.float32)
    r2 = np.zeros((128, 130), np.float32)
    r2[:, :128] = gc * ((a <= b) & same)
    r2[:, 128] = gc * (i < 64)
    r2[:, 129] = gc * (i >= 64)
    c["gr2"] = r2
    c["mask_gla"] = ((a <= b) & same).astype(np.float32)
    c["bdones"] = same.astype(np.float32)
    return c


CONST_SHAPES = {"ident": (128, 128), "triinc_neg": (128, 128), "mask_strict": (128, 128),
                "gl1": (128, 128), "gr2": (128, 130), "mask_gla": (128, 128), "bdones": (128, 128)}


class StopBuild(Exception):
    pass


def build(nseq, stop_after=None, dbg=()):
    try:
        return _build(nseq, stop_after, dbg)
    except StopBuild as e:
        return e.args[0]


def _build(nseq, stop_after=None, dbg=()):
    nc = bass.Bass("TRN2", target_bir_lowering=False)
    import os as _os
    _stopk = int(_os.environ.get("STOPK", "0"))
    _ckn = [0]
    dbg = dict(dbg)
    dbg_out = {}

    def din(name, shape, dt=F32):
        return nc.dram_tensor(name, list(shape), dt, kind="ExternalInput").ap()

    x_d = din("x", [nseq, SEQ, D])
    w_in_d = din("w_in", [D, 3088])
    w_out_d = din("w_out", [D, D])
    w_up_d = din("w_up", [D, 2 * DFF])
    w_dn_d = din("w_dn", [DFF, D])
    wgu_d = din("wgu", [16, 256])
    bgu_d = din("bgu", [1, 256])
    gb1_d = din("gb1", [128, D])
    gb2_d = din("gb2", [128, D])
    fing_d = din("fing", [128, D])
    gsb_d = din("gsb", [128, 4])
    ggl_d = din("ggl", [128, 4])
    cw_d = din("cw", [128, 3 * 44])
    cb_d = din("cb", [128, 44])
    cst_d = {k: din("c_" + k, v) for k, v in CONST_SHAPES.items()}
    out_d = nc.dram_tensor("out", [nseq, SEQ, D], F32, kind="ExternalOutput").ap()
    for name, shape in dbg.items():
        dbg_out[name] = nc.dram_tensor("dbg_" + name, list(shape), F32, kind="ExternalOutput").ap()

    w_in_v = w_in_d.rearrange("(kt p) c -> p kt c", p=128)
    w_up_v = w_up_d.rearrange("(kt p) c -> p kt c", p=128)
    w_dn_v = w_dn_d.rearrange("(j p) c -> p j c", p=128)
    w_out_sb_v = w_out_d[0:512, :].rearrange("(h p) c -> p h c", p=128)
    w_out_gl_v = w_out_d[512:1024, :].rearrange("(h p) c -> p h c", p=128)

    top = ExitStack()
    S = Sched(nc, top)
    manual = []

    def ck(tag=""):
        _ckn[0] += 1
        if _stopk and _ckn[0] == _stopk:
            print("STOP at checkpoint", _ckn[0], tag, flush=True)
            S.flush()
            raise StopBuild(nc)

    uid = [0]

    def sb(stack, name, shape, dt, side=None):
        uid[0] += 1
        name = f"t{uid[0]}_{name}"
        if side is None:
            return stack.enter_context(nc.sbuf_tensor(name, list(shape), dt))
        return stack.enter_context(nc.sbuf_tensor(name, list(shape), dt, side=side))

    def ACT(out, in_, func, r, w, **kw):
        S.op('act', lambda e: e.activation(out=out, in_=in_, func=func, **kw), r, w)

    def TS(eng, out, in0, s1, s2, op0, op1, r, w):
        if s2 is None:
            S.op(eng, lambda e: e.tensor_scalar(out=out, in0=in0, scalar1=s1, scalar2=None, op0=op0), r, w)
        else:
            S.op(eng, lambda e: e.tensor_scalar(out=out, in0=in0, scalar1=s1, scalar2=s2, op0=op0, op1=op1), r, w)

    def TT(eng, out, in0, in1, op, r, w):
        S.op(eng, lambda e: e.tensor_tensor(out=out, in0=in0, in1=in1, op=op), r, w)

    def STT(eng, out, in0, scalar, in1, op0, op1, r, w):
        S.op(eng, lambda e: e.scalar_tensor_tensor(out=out, in0=in0, scalar=scalar, in1=in1, op0=op0, op1=op1), r, w)

    def CP(eng, out, in_, r, w):
        S.op(eng, lambda e: e.tensor_copy(out=out, in_=in_), r, w)

    def MS(eng, ap, val, w):
        S.op(eng, lambda e: e.memset(ap, val), (), w)

    def MM(out, lhsT, rhs, start, stop, r, w, sgc=False):
        if sgc:
            S.op('pe', lambda e: e.matmul(out, lhsT, rhs, start=start, stop=stop, skip_group_check=True), r, w)
        else:
            S.op('pe', lambda e: e.matmul(out, lhsT, rhs, start=start, stop=stop), r, w)

    def TR(out, in_, ident, r, w):
        S.op('pe', lambda e: e.transpose(out, in_, ident), r, w)

    def DMA(eng, out, in_, r, w):
        S.dma(eng, lambda e: e.dma_start(out=out, in_=in_), r, w)

    def dump(name, ap, keys, rows=None):
        if name in dbg_out:
            DMA('sp', dbg_out[name] if rows is None else dbg_out[name][rows], ap, keys, ())

    ps2 = [top.enter_context(nc.psum_tensor(f"psp{i}", [128, 1024], F32)) for i in range(4)]
    ps = []
    for i_ in range(4):
        ps.append(ps2[i_][:, 0:512])
        ps.append(ps2[i_][:, 512:1024])
    psb = [ps[i][:, :].bitcast(BF) for i in range(2)]

    ident = sb(top, "ident", [128, 128], BF)
    triinc = sb(top, "triinc", [128, 128], BF)
    onesneg = sb(top, "onesneg", [128, 128], BF)
    mstrict = sb(top, "mstrict", [128, 128], BF)
    gl1 = sb(top, "gl1", [128, 128], F32)
    gr2 = sb(top, "gr2", [128, 130], F32)
    mgla = sb(top, "mgla", [128, 128], F32)
    onesf = sb(top, "onesf", [128, 128], F32)
    onesrow = sb(top, "onesrow", [1, 128], BF)
    bgu = sb(top, "bgu", [1, 256], BF)
    wgu = sb(top, "wgu", [16, 256], BF)
    gb1 = sb(top, "gb1", [128, D], F32)
    gb2 = sb(top, "gb2", [128, D], F32)
    fing = sb(top, "fing", [128, D], F32)
    gsb = sb(top, "gsb", [128, 4], F32)
    bdones = sb(top, "bdones", [128, 128], BF)
    onesb = sb(top, "onesb", [128, 128], BF)
    ggl = sb(top, "ggl", [128, 4], F32)
    cw = sb(top, "cw", [128, 3 * 44], F32)
    cb = sb(top, "cb", [128, 44], F32)
    ss = sb(top, "ss", [128, 16], F32)
    rs = sb(top, "rs", [128, 16], F32)

    DMA('pool', ident[:, :], cst_d["ident"], (), ['ident'])
    DMA('pool', triinc[:, :], cst_d["triinc_neg"], (), ['triinc'])
    DMA('pool', mstrict[:, :], cst_d["mask_strict"], (), ['mstrict'])
    DMA('pool', bgu[:, :], bgu_d, (), ['bgu'])
    DMA('pool', wgu[:, :], wgu_d, (), ['wgu'])
    DMA('sp', gl1[:, :], cst_d["gl1"], (), ['gl1'])
    DMA('sp', gr2[:, :], cst_d["gr2"], (), ['gr2'])
    DMA('sp', mgla[:, :], cst_d["mask_gla"], (), ['mgla'])
    DMA('sp', gb1[:, :], gb1_d, (), ['gb1'])
    DMA('sp', gb2[:, :], gb2_d, (), ['gb2'])
    DMA('sp', fing[:, :], fing_d, (), ['fing'])
    DMA('sp', gsb[:, :], gsb_d, (), ['gsb'])
    DMA('pool', bdones[:, :], cst_d["bdones"], (), ['bdones'])
    MS('dve', onesb[:, :], 1.0, ['onesb'])
    DMA('sp', ggl[:, :], ggl_d, (), ['ggl'])
    DMA('sp', cw[:, :], cw_d, (), ['cw'])
    DMA('sp', cb[:, :], cb_d, (), ['cb'])
    MS('dve', onesneg[:, :], -1.0, ['onesneg'])
    MS('dve', onesf[:, :], 1.0, ['onesf'])
    MS('dve', onesrow[:, :], 1.0, ['onesrow'])
    S.flush()

    def norm_T(st, srcs, gbt, dstT, dkey):
        xn = [sb(st, f"xn{i}", [128, D], BF) for i in range(2)]
        junk = sb(st, "junk", [128, D], BF)
        n = len(srcs)
        MS('dve', ss[:, 0:n], 0.0, ['ss'])
        for i, (src, skeys) in enumerate(srcs):
            b = i % 2
            ACT(junk[:, :], src, AF.Square, skeys, ['junk', 'ss'], accum_out=ss[:, i:i + 1])
            ACT(rs[:, i:i + 1], ss[:, i:i + 1], AF.Ln, ['ss'], [('rs', i)], scale=1.0 / D, bias=EPS)
            ACT(rs[:, i:i + 1], rs[:, i:i + 1], AF.Exp, [('rs', i)], [('rs', i)], scale=-0.5)
            TS('dve', xn[b][:, :], src, rs[:, i:i + 1], None, ALU.mult, None, list(skeys) + [('rs', i)], [('xn', b)])
            for kt in range(8):
                TR(psb[b][:, kt * 128:(kt + 1) * 128], xn[b][:, kt * 128:(kt + 1) * 128], ident[:, :],
                   [('xn', b), 'ident'], [('ps', b)])
            TT('dve', dstT[:, :, i * 128:(i + 1) * 128],
               psb[b].rearrange("p (k t) -> p k t", k=8),
               gbt[:, :].rearrange("p (k t) -> p k t", k=8), ALU.mult,
               [('ps', b), 'gbt'], [(dkey, i)])

    def hkeys(key, c0, c1):
        return [(key, i) for i in range(c0 // 128, (c1 + 127) // 128)]

    pj = [0]

    def proj_fm(W, wkey, wc0, M, srcT, skey, ntok, evac):
        for tg in range(ntok // 512):
            bank = 2 + (pj[0] % 2)
            pj[0] += 1
            for kt in range(8):
                MM(ps[bank][0:M, :], W[:, kt, wc0:wc0 + M], srcT[:, kt, tg * 512:(tg + 1) * 512],
                   kt == 0, kt == 7, [wkey] + hkeys(skey, tg * 512, tg * 512 + 512), [('ps', bank)])
            evac(tg, ps[bank][0:M, :], [('ps', bank)])

    carry = sb(top, "carry", [128, 44, 2], F32)
    SS16 = [('ss', i) for i in range(16)]

    for seq in range(nseq):
        sA = ExitStack()
        osT = sb(sA, "osT", [128, 4, SEQ], BF)
        ogT = sb(sA, "ogT", [128, 4, SEQ], BF)
        sB = ExitStack()
        hT = sb(sB, "hT", [128, 8, SEQ], BF)

        with ExitStack() as st:
            wk = sb(st, "wk", [128, 8, 256], BF)
            wq = sb(st, "wq", [128, 8, 256], BF)
            wog = sb(st, "wog", [128, 8, 512], BF)
            vtok = sb(st, "vtok", [128, 16, 512], BF)
            kend = sb(st, "kend", [128, 16, 256], BF)
            sptok = sb(st, "sptok", [128, 16, 256], F32)
            with ExitStack() as stA_:
                wv = sb(stA_, "wv", [128, 8, 512], BF)
                wlr = sb(stA_, "wlr", [128, 8, 16], BF)
                glrT = sb(stA_, "glrT", [16, SEQ], BF)
                e1 = [sb(stA_, f"e1{i}", [128, 256], F32) for i in range(2)]
                erb = [sb(stA_, f"erb{i}", [128, 256], F32) for i in range(2)]
                DMA('pool', wlr[:, :, :], w_in_v[:, :, 2560:2576], (), ['wlr'])
                DMA('pool', wk[:, :, :], w_in_v[:, :, 1792:2048], (), ['wk'])
                DMA('pool', wv[:, :, :], w_in_v[:, :, 2048:2560], (), ['wv'])
                DMA('pool', wq[:, :, :], w_in_v[:, :, 1536:1792], (), ['wq'])
                DMA('pool', wog[:, :, :], w_in_v[:, :, 2576:3088], (), ['wog'])
                xt = [sb(stA_, f"xt{i}", [128, D], F32) for i in range(4)]
                xn = [sb(stA_, f"xn{i}", [128, D], BF) for i in range(2)]
                junk = sb(stA_, "junk", [128, D], BF)
                MS('dve', ss[:, 0:16], 0.0, SS16)

                def p1_tile(i):
                    b = i % 2
                    xb = i % 4
                    DMA('sp', xt[xb][:, :], x_d[seq, i * 128:(i + 1) * 128, :], (), [('xt', xb)])
                    src = xt[xb][:, :]
                    skeys = [('xt', xb)]
                    ACT(junk[:, :], src, AF.Square, skeys, ['junk', ('ss', i)], accum_out=ss[:, i:i + 1])
                    ACT(rs[:, i:i + 1], ss[:, i:i + 1], AF.Ln, [('ss', i)], [('rs', i)], scale=1.0 / D, bias=EPS)
                    ACT(rs[:, i:i + 1], rs[:, i:i + 1], AF.Exp, [('rs', i)], [('rs', i)], scale=-0.5)
                    TS('dve', xn[b][:, :], src, rs[:, i:i + 1], None, ALU.mult, None, skeys + [('rs', i)], [('xn', b)])
                    for kt in range(8):
                        TR(psb[b][:, kt * 128:(kt + 1) * 128], xn[b][:, kt * 128:(kt + 1) * 128], ident[:, :],
                           [('xn', b)], [('ps', b)])
                    TT('dve', hT[:, :, i * 128:(i + 1) * 128],
                       psb[b].rearrange("p (k t) -> p k t", k=8),
                       gb1[:, :].rearrange("p (k t) -> p k t", k=8), ALU.mult,
                       [('ps', b)], [('hT', i)])

                def glr_group(tg):
                    bank = 2 + (pj[0] % 2)
                    pj[0] += 1
                    for kt in range(8):
                        MM(ps[bank][0:16, :], wlr[:, kt, 0:16], hT[:, kt, tg * 512:(tg + 1) * 512],
                           kt == 0, kt == 7, ['wlr'] + hkeys('hT', tg * 512, tg * 512 + 512), [('ps', bank)])
                    ACT(glrT[0:16, tg * 512:(tg + 1) * 512], ps[bank][0:16, :], AF.Copy, [('ps', bank)], [('glrT', tg)])

                def glaA_tile(tt):
                    b = tt % 2
                    tsl = slice(tt * 128, (tt + 1) * 128)
                    bp, brb = (4, 5) if b == 0 else (2, 3)
                    bk_, bv_ = (6, 7)
                    MM(ps[bp][:, 0:256], glrT[0:16, tsl], wgu[0:16, :], True, False, [('glrT', tt // 4)], [('ps', bp)])
                    MM(ps[bp][:, 0:256], onesrow[0:1, :], bgu[0:1, :], False, True, [], [('ps', bp)])
                    ACT(e1[b][:, :], ps[bp][:, 0:256], AF.Exp, [('ps', bp)], [('e1', b)], scale=-1.0)
                    ACT(sptok[:, tt, :], e1[b][:, :], AF.Ln, [('e1', b)], [('sp', tt)], bias=1.0)
                    MM(ps[brb][:, 0:256], gl1[:, :], sptok[:, tt, :], True, True, [('sp', tt)], [('ps', brb)])
                    ACT(erb[b][:, :], ps[brb][:, 0:256], AF.Exp, [('ps', brb)], [('erb', b)])
                    for kt in range(8):
                        MM(ps[bk_][:, 0:256], hT[:, kt, tsl], wk[:, kt, :], kt == 0, kt == 7, [('hT', tt), 'wk'], [('ps', bk_)])
                    TT('dve', kend[:, tt, :], ps[bk_][:, 0:256], erb[b][:, :], ALU.mult, [('ps', bk_), ('erb', b)], [('kend', tt)])
                    for kt in range(8):
                        MM(ps[bv_][:, :], hT[:, kt, tsl], wv[:, kt, :], kt == 0, kt == 7, [('hT', tt), 'wv'], [('ps', bv_)])
                    CP('dve', vtok[:, tt, :], ps[bv_][:, :], [('ps', bv_)], [('vtok', tt)])

                for i in range(NTT + 5):
                    if i < NTT:
                        p1_tile(i)
                    if 1 <= i <= NTT and i % 4 == 0:
                        glr_group(i // 4 - 1)
                    if i >= 5:
                        glaA_tile(i - 5)
                S.flush()

            gq = [sb(st, f"gq{i}", [64, SEQ], BF) for i in range(2)]
            gk = [sb(st, f"gk{i}", [64, SEQ], BF) for i in range(2)]
            sog = [sb(st, f"sog{i}", [128, SEQ], BF) for i in range(2)]
            Sall = [sb(st, f"Sall{i}", [64, 32, 128], BF) for i in range(2)]
            Sf = [[sb(st, f"Sf{i}{j}", [64, 128], F32) for j in range(2)] for i in range(2)]
            dec = [sb(st, f"dec{i}", [64, 32], F32) for i in range(2)]
            eb = [[sb(st, f"eb{i}{j}", [64, 128], F32) for j in range(2)] for i in range(2)]
            enb = [[sb(st, f"enb{i}{j}", [64, 128], F32) for j in range(2)] for i in range(2)]
            attn = [[sb(st, f"attn{i}{j}", [128, 128], BF) for j in range(2)] for i in range(2)]
            osq = [sb(st, f"osq{i}", [128, 512], BF) for i in range(2)]
            of = [sb(st, f"of{i}", [128, 512], F32) for i in range(2)]
            rst = [sb(st, f"rst{i}", [128, 512], F32) for i in range(2)]
            if seq == 0:
                print("GLA sbuf bytes remaining", nc.sbuf_bytes_remaining, flush=True)

            for pp in range(2):
                for q in range(2):
                    h = 2 * pp + q

                    def ev_q(tg, p, pk, q=q):
                        ACT(gq[q][0:64, tg * 512:(tg + 1) * 512], p, AF.Copy, pk, [('gq', q, tg)], scale=0.125)

                    def ev_k(tg, p, pk, q=q):
                        ACT(gk[q][0:64, tg * 512:(tg + 1) * 512], p, AF.Copy, pk, [('gk', q, tg)])

                    def ev_og(tg, p, pk, q=q):
                        ACT(sog[q][:, tg * 512:(tg + 1) * 512], p, AF.Silu, pk, [('sog', q, tg)])
                    proj_fm(wq, 'wq', 64 * h, 64, hT, 'hT', SEQ, ev_q)
                    proj_fm(wk, 'wk', 64 * h, 64, hT, 'hT', SEQ, ev_k)
                    proj_fm(wog, 'wog', 128 * h, 128, hT, 'hT', SEQ, ev_og)
                for tt in range(NTT):
                    b = tt % 2
                    tg = tt // 4
                    tsl = slice(tt * 128, (tt + 1) * 128)
                    for q in range(2):
                        h = 2 * pp + q
                        bb = 4 + q
                        MM(ps[bb][0:64, 0:130], sptok[:, tt, 64 * h:64 * h + 64], gr2[:, :], True, True, [('sp', tt)], [('ps', bb)])
                        ACT(eb[q][b][:, :], ps[bb][0:64, 0:128], AF.Exp, [('ps', bb)], [('eb', q, b)])
                        ACT(enb[q][b][:, :], ps[bb][0:64, 0:128], AF.Exp, [('ps', bb)], [('enb', q, b)], scale=-1.0)
                        ACT(dec[q][:, 2 * tt:2 * tt + 2], ps[bb][0:64, 128:130], AF.Exp, [('ps', bb)], [('dec', q, tt)])
                        TT('dve', gq[q][0:64, tsl], gq[q][0:64, tsl], eb[q][b][:, :], ALU.mult, [('gq', q, tg), ('eb', q, b)], [('gq', q, tg)])
                        TT('dve', gk[q][0:64, tsl], gk[q][0:64, tsl], enb[q][b][:, :], ALU.mult, [('gk', q, tg), ('enb', q, b)], [('gk', q, tg)])
                for q in range(2):
                    MS('dve', Sf[q][0][:, :], 0.0, [('Sf', q, 0)])
                    MS('dve', Sall[q][:, 0, :], 0.0, [('Sall', q, 0)])
                for n in range(32):
                    tt, half = n // 2, n % 2
                    pr = slice(64 * half, 64 * half + 64)
                    for q in range(2):
                        h = 2 * pp + q
                        kbank = ((2, 3), (6, 7))[q][n % 2]
                        kvp = ps[kbank][0:64, 0:128]
                        MM(kvp, kend[pr, tt, 64 * h:64 * h + 64], vtok[pr, tt, 128 * h:128 * h + 128], True, True,
                           [('kend', tt), ('vtok', tt)], [('ps', kbank)])
                        cur, prev = Sf[q][(n + 1) % 2], Sf[q][n % 2]
                        STT('dve', cur[:, :], prev[:, :], dec[q][:, n:n + 1], kvp, ALU.mult, ALU.add,
                            [('Sf', q, n % 2), ('dec', q, tt), ('ps', kbank)], [('Sf', q, (n + 1) % 2)])
                        if n < 31:
                            ACT(Sall[q][:, n + 1, :], cur[:, :], AF.Copy, [('Sf', q, (n + 1) % 2)], [('Sall', q, n + 1)])
                gdef = []

                def g_run():
                    for d_ in list(gdef):
                        d_[0] -= 1
                    for d_ in list(gdef):
                        if d_[0] <= 0:
                            d_[1]()
                            gdef.remove(d_)

                for tg in range(4):
                    gsl = slice(tg * 512, (tg + 1) * 512)
                    for qq in range(4):
                        tt = tg * 4 + qq
                        b = tt % 2
                        tsl = slice(tt * 128, (tt + 1) * 128)
                        for q in range(2):
                            h = 2 * pp + q
                            ab = ps[6 + q]
                            abk = ('ps', 6 + q)
                            obn = q if tg % 2 == 0 else 4 + q
                            ob = ps[obn]
                            obk = ('ps', obn)
                            MM(ab[:, 0:128], gk[q][0:64, tsl], gq[q][0:64, tsl], True, True, [('gk', q, tg), ('gq', q, tg)], [abk])
                            TT('dve', attn[q][b][:, :], ab[:, 0:128], mgla[:, :], ALU.mult, [abk], [('attn', q, b)])
                            MM(ob[:, qq * 128:(qq + 1) * 128], vtok[:, tt, 128 * h:128 * h + 128], attn[q][b][:, :], qq == 0, False,
                               [('vtok', tt), ('attn', q, b)], [obk], sgc=True)
                            MM(ob[:, qq * 128:qq * 128 + 64], Sall[q][:, 2 * tt, :], gq[q][0:64, tt * 128:tt * 128 + 64], False, False,
                               [('Sall', q, 2 * tt), ('gq', q, tg)], [obk], sgc=True)
                            MM(ob[:, qq * 128 + 64:qq * 128 + 128], Sall[q][:, 2 * tt + 1, :], gq[q][0:64, tt * 128 + 64:tt * 128 + 128], False, True,
                               [('Sall', q, 2 * tt + 1), ('gq', q, tg)], [obk], sgc=True)
                        g_run()
                    for q in range(2):
                        def mkhn(q=q, tg=tg, gsl=gsl):
                            h = 2 * pp + q
                            obn = q if tg % 2 == 0 else 4 + q
                            ob = ps[obn]
                            obk = ('ps', obn)

                            def h1():
                                CP('dve', of[q][:, :], ob[:, :], [obk], [('of', q)])
                                ACT(osq[q][:, :], of[q][:, :], AF.Square, [('of', q)], [('osq', q)])

                            def h2():
                                MM(ps[2 + q][:, :], onesb[:, :], osq[q][:, :], True, True, [('osq', q)], [('ps', 2 + q)])

                            def h3():
                                ACT(rst[q][:, :], ps[2 + q][:, :], AF.Ln, [('ps', 2 + q)], [('rst', q)], scale=1.0 / 128, bias=EPS)
                                ACT(rst[q][:, :], rst[q][:, :], AF.Exp, [('rst', q)], [('rst', q)], scale=-0.5)

                            def h4():
                                STT('dve', of[q][:, :], of[q][:, :], ggl[:, h:h + 1], rst[q][:, :], ALU.mult, ALU.mult, [('of', q), ('rst', q)], [('of', q)])
                                TT('dve', ogT[:, h, gsl], of[q][:, :], sog[q][:, gsl], ALU.mult, [('of', q), ('sog', q, tg)], [('ogT', h, tg)])
                            h1()
                            gdef.append([1, h2])
                            gdef.append([2, h3])
                            gdef.append([3, h4])
                        mkhn()
                while gdef:
                    g_run()
            S.flush()
            if 'ogT' in dbg_out:
                DMA('pool', dbg_out['ogT'], ogT[:, :, :], [], ())
                S.flush()
        if stop_after == 'GLA':
            sB.close(); sA.close()
            break

        with ExitStack() as st:
            wq = sb(st, "wsq", [128, 8, 512], BF)
            wk = sb(st, "wsk", [128, 8, 512], BF)
            wv = sb(st, "wsv", [128, 8, 512], BF)
            DMA('pool', wv[:, :, :], w_in_v[:, :, 1024:1536], (), ['wsv'])
            DMA('pool', wq[:, :, :], w_in_v[:, :, 0:512], (), ['wsq'])
            DMA('pool', wk[:, :, :], w_in_v[:, :, 512:1024], (), ['wsk'])
            vtok = sb(st, "vts", [128, 16, 512], BF)
            sqp = [sb(st, f"sq{i}", [128, SEQ], BF) for i in range(2)]
            skp = [[sb(st, f"sk{i}{j}", [128, SEQ], BF) for j in range(2)] for i in range(2)]
            for i_ in range(2):
                MS('pool', skp[i_][0][64:128, :], 0.0, [('skz', i_, 0)])
                MS('pool', skp[i_][1][0:64, :], 0.0, [('skz', i_, 1)])
            esbI = [sb(st, f"esb{i}", [128, 1024], BF) for i in range(3)]
            spbI = [sb(st, f"spb{i}", [128, 1024], BF) for i in range(3)]
            wsbI = [sb(st, f"wsb{i}", [128, 1024], BF) for i in range(3)]
            Ls = [sb(st, f"Ls{i}", [128, 512], BF) for i in range(2)]
            osq = sb(st, "osq", [128, 512], BF)
            of = sb(st, "of", [128, 512], F32)
            rst = sb(st, "rst", [128, 512], F32)
            MS('dve', osq[:, :], 0.0, ['osq'])
            def v_chunk(tt):
                bank = 5
                tsl = slice(tt * 128, (tt + 1) * 128)
                for kt in range(8):
                    MM(ps[bank][:, :], hT[:, kt, tsl], wv[:, kt, :], kt == 0, kt == 7, [('hT', tt), 'wsv'], [('ps', bank)])
                CP('dve', vtok[:, tt, :], ps[bank][:, :], [('ps', bank)], [('vts', tt)])
            for tt in range(10):
                v_chunk(tt)
            hold = [0]
            tiles = [(h, G, kb) for h in range(8) for G in range(4) for kb in range(4 * G + 3, -1, -1)]
            NT = len(tiles)

            pending = []

            def sb_proj_chunks(P):
                pp = P % 2
                out = []
                for which in ('q', 'k'):
                    W = wq if which == 'q' else wk
                    for tg in range(4):
                        def mk(W=W, which=which, tg=tg):
                            state = {}
                            tsl_ = slice(tg * 512, (tg + 1) * 512)

                            def c1():
                                bank = 5
                                state['bank'] = bank
                                for kt in range(4):
                                    MM(ps[bank][:, :], W[:, kt, 128 * P:128 * P + 128], hT[:, kt, tsl_],
                                       kt == 0, False, ['wsq' if which == 'q' else 'wsk'], [('ps', bank)])

                            def c2():
                                bank = state['bank']
                                for kt in range(4, 8):
                                    MM(ps[bank][:, :], W[:, kt, 128 * P:128 * P + 128], hT[:, kt, tsl_],
                                       False, kt == 7, ['wsq' if which == 'q' else 'wsk'], [('ps', bank)])
                                if which == 'q':
                                    TS('dve', sqp[pp][:, tsl_], ps[bank][:, :], 0.125, None, ALU.mult, None,
                                       [('ps', bank)], [('sq', pp, tg)])
                                else:
                                    CP('dve', skp[pp][0][0:64, tsl_], ps[bank][0:64, :], [('ps', bank)], [('sk', pp, 0, tg)])
                                    CP('dve', skp[pp][1][64:128, tsl_], ps[bank][64:128, :], [('ps', bank)], [('sk', pp, 1, tg)])
                            return [c1, c2]
                        out.extend(mk())
                return out

            items = []
            for h in range(8):
                for G in range(4):
                    for j in range(3, -1, -1):
                        items.append(dict(h=h, G=G, tl=[(4 * G + j, 128 * j)], diag=True, first=(j == 3), last=(G == 0 and j == 0)))
                    for kb in range(4 * G - 1, -1, -2):
                        items.append(dict(h=h, G=G, tl=[(kb, 0), (kb - 1, 0)], diag=False, first=False, last=(kb - 1 == 0)))
            NI = len(items)
            ZP = [(ps2[0], 0, 1), (ps2[1], 2, 3), (ps2[3], 6, 7)]
            OBN = 4

            def iinfo(i):
                t = dict(items[i])
                h, G = t['h'], t['G']
                pp = (h // 2) % 2
                t.update(pp=pp, sq=sqp[pp], sk=skp[pp][h % 2], qkey=('sq', pp, G), zkey=('skz', pp, h % 2),
                         lk=('L', G % 2), L=Ls[G % 2], zt=ZP[i % 3][0], zkeys=[('ps', ZP[i % 3][1]), ('ps', ZP[i % 3][2])],
                         lo=t['tl'][0][1], hi=512 * len(t['tl']), e=esbI[i % 3], sp=spbI[i % 3], w=wsbI[i % 3],
                         ek=('esb', i % 3), spk=('spb', i % 3), wk_=('wsb', i % 3))
                return t

            def stA(i):
                t = iinfo(i)
                if t['first']:
                    if t['G'] == 0 and t['h'] % 2 == 0:
                        assert not (pending and hold[0])
                        while pending:
                            pending.pop(0)()
                        if t['h'] // 2 + 1 < 4:
                            pending.extend(sb_proj_chunks(t['h'] // 2 + 1))
                    MS('pool', t['L'][:, :], 0.0, [t['lk']])
                elif pending and not t['diag'] and not hold[0]:
                    pending.pop(0)()
                    if pending and len(pending) % 2 == 1:
                        pending.pop(0)()
                zt = t['zt']
                for ti, (kb, c0) in enumerate(t['tl']):
                    MM(zt[:, ti * 512 + c0:ti * 512 + 512], t['sk'][:, kb * 128:(kb + 1) * 128],
                       t['sq'][:, 512 * t['G'] + c0:512 * t['G'] + 512], True, False,
                       [('sk', t['pp'], t['h'] % 2, kb // 4), t['qkey'], t['zkey']], t['zkeys'], sgc=True)
                lo, hi = t['lo'], t['hi']
                ACT(t['e'][:, lo:hi], zt[:, lo:hi], AF.Exp, t['zkeys'], [t['ek']])
                ACT(t['sp'][:, lo:hi], t['e'][:, lo:hi], AF.Ln, [t['ek']], [t['spk']], bias=1.0)
                if t['diag']:
                    TT('dve', t['sp'][:, lo:lo + 128], t['sp'][:, lo:lo + 128], mstrict[:, :], ALU.mult, [t['spk']], [t['spk']])

            def stB(i):
                t = iinfo(i)
                zt = t['zt']
                L = t['L']
                lo, hi = t['lo'], t['hi']
                kb0, c0 = t['tl'][0]
                MM(zt[:, c0:512], triinc[:, :], t['sp'][:, c0:512], False, t['first'], [t['spk']], t['zkeys'], sgc=True)
                if not t['first']:
                    MM(zt[:, c0:512], onesneg[:, :], L[:, c0:512], False, True, [t['lk']], t['zkeys'], sgc=True)
                if len(t['tl']) == 2:
                    MM(zt[:, 512:1024], triinc[:, :], t['sp'][:, 512:1024], False, False, [t['spk']], t['zkeys'], sgc=True)
                    MM(zt[:, 512:1024], onesneg[:, :], L[:, :], False, False, [t['lk']], t['zkeys'], sgc=True)
                    MM(zt[:, 512:1024], onesneg[:, :], t['sp'][:, 0:512], False, True, [t['spk']], t['zkeys'], sgc=True)
                ACT(t['w'][:, lo:hi], zt[:, lo:hi], AF.Exp, t['zkeys'], [t['wk_']])
                if t['diag']:
                    TT('dve', t['w'][:, lo:lo + 128], t['w'][:, lo:lo + 128], mstrict[:, :], ALU.mult, [t['wk_']], [t['wk_']])
                if not t['last']:
                    TT('dve', L[:, c0:512], L[:, c0:512], t['sp'][:, c0:512], ALU.add, [t['lk'], t['spk']], [t['lk']])
                    if len(t['tl']) == 2:
                        TT('dve', L[:, :], L[:, :], t['sp'][:, 512:1024], ALU.add, [t['lk'], t['spk']], [t['lk']])

            def stC(i):
                t = iinfo(i)
                h, G = t['h'], t['G']
                ob, obk = ps[OBN], ('ps', OBN)
                pr = h // 2
                for ti, (kb, c0) in enumerate(t['tl']):
                    MM(ob[:, c0:512], vtok[:, kb, 128 * pr:128 * pr + 128], t['w'][:, ti * 512 + c0:ti * 512 + 512],
                       t['first'] and ti == 0, t['last'] and ti == len(t['tl']) - 1,
                       [('vts', kb), t['wk_']], [obk], sgc=True)
                if t['last']:
                    gsl = slice(G * 512, (G + 1) * 512)
                    hp = 64 * (h % 2)
                    rows = slice(hp, hp + 64)

                    def hn1():
                        CP('dve', of[rows, :], ob[rows, :], [obk], [('of', hp)])
                        TT('dve', osq[rows, :], of[rows, :], of[rows, :], ALU.mult, [('of', hp)], ['osq'])

                    def hn2():
                        while pending and len(pending) % 2 == 1:
                            pending.pop(0)()
                        hold[0] = 1
                        MM(ps[5][:, :], bdones[:, :], osq[:, :], True, True, ['osq'], [('ps', 5)])

                    def hn3():
                        ACT(rst[rows, :], ps[5][rows, :], AF.Ln, [('ps', 5)], [('rst', hp)], scale=1.0 / 64, bias=EPS)
                        ACT(rst[rows, :], rst[rows, :], AF.Exp, [('rst', hp)], [('rst', hp)], scale=-0.5)
                        hold[0] = 0

                    def hn4():
                        STT('dve', osT[rows, pr, gsl], of[rows, :], gsb[rows, pr:pr + 1], rst[rows, :], ALU.mult, ALU.mult,
                            [('of', hp), ('rst', hp)], [('osT', h, G)])
                    hn1()
                    deferred.append([1, hn2])
                    deferred.append([2, hn3])
                    deferred.append([3, hn4])

            deferred = []

            def run_deferred(all_=False):
                for d_ in list(deferred):
                    d_[0] -= 1
                for d_ in list(deferred):
                    if d_[0] <= 0 or all_:
                        d_[1]()
                        deferred.remove(d_)

            for c_ in sb_proj_chunks(0):
                c_()
            for k in range(NI + 2):
                if 10 + k < NTT:
                    v_chunk(10 + k)
                if k < NI:
                    stA(k)
                if 1 <= k <= NI:
                    stB(k - 1)
                run_deferred()
                if k >= 2:
                    stC(k - 2)
            while deferred:
                run_deferred()
            S.flush()
            if 'osT' in dbg_out:
                DMA('pool', dbg_out['osT'], osT[:, :, :], [], ())
                S.flush()
        sB.close()
        if stop_after == 'SB':
            sA.close()
            break

        sR = ExitStack()
        x1 = sb(sR, "x1", [128, NTT, D], F32, side='right')
        h2T = sb(sR, "h2T", [128, 8, 1024], BF, side='right')
        xn = [sb(sR, f"xn{i}", [128, D], BF, side='right') for i in range(2)]
        junk = sb(sR, "junk", [128, D], BF, side='right')

        def n2_front(hf_, i):
            c = hf_ * 8 + i
            tt_ = hf_ * 8 + i
            b_ = i % 2
            src = x1[:, tt_, :]
            skeys = [('x1', tt_)]
            ACT(junk[:, :], src, AF.Square, skeys, ['junk', ('ss', c)], accum_out=ss[:, c:c + 1])
            ACT(rs[:, c:c + 1], ss[:, c:c + 1], AF.Ln, [('ss', c)], [('rs', c)], scale=1.0 / D, bias=EPS)
            ACT(rs[:, c:c + 1], rs[:, c:c + 1], AF.Exp, [('rs', c)], [('rs', c)], scale=-0.5)
            TS('dve', xn[b_][:, :], src, rs[:, c:c + 1], None, ALU.mult, None, skeys + [('rs', c)], [('xn', b_)])

        def n2_back(hf_, i):
            b_ = i % 2
            for kt in range(8):
                TR(psb[b_][:, kt * 128:(kt + 1) * 128], xn[b_][:, kt * 128:(kt + 1) * 128], ident[:, :],
                   [('xn', b_)], [('ps', b_)])
            TT('dve', h2T[:, :, i * 128:(i + 1) * 128],
               psb[b_].rearrange("p (k t) -> p k t", k=8),
               gb2[:, :].rearrange("p (k t) -> p k t", k=8), ALU.mult,
               [('ps', b_)], [('h2T', i)])


        with ExitStack() as st:
            wos = sb(st, "wos", [128, 4, D], BF)
            wogl = sb(st, "wogl", [128, 4, D], BF)
            DMA('pool', wos[:, :, :], w_out_sb_v, (), ['wos'])
            DMA('pool', wogl[:, :, :], w_out_gl_v, (), ['wogl'])
            xt = [sb(st, f"xt{i}", [128, D], F32) for i in range(2)]
            cnt = 0
            MS('dve', ss[:, 0:16], 0.0, SS16)
            for tt in range(NTT):
                b = tt % 2
                tsl = slice(tt * 128, (tt + 1) * 128)
                DMA('sp', xt[b][:, :], x_d[seq, tsl, :], (), [('xt', b)])
                for cg in range(2):
                    bank = 2 + cnt % 4
                    cnt += 1
                    csl = slice(cg * 512, (cg + 1) * 512)
                    for h in range(4):
                        MM(ps[bank][:, :], osT[:, h, tsl], wos[:, h, csl], h == 0, False, ['wos'], [('ps', bank)])
                    for h in range(4):
                        MM(ps[bank][:, :], ogT[:, h, tsl], wogl[:, h, csl], False, h == 3, ['wogl'], [('ps', bank)])
                    TT('dve', x1[:, tt, csl], ps[bank][:, :], xt[b][:, csl], ALU.add, [('ps', bank), ('xt', b)], [('x1', tt)])
                if tt < 8:
                    n2_front(0, tt)
                if 1 <= tt <= 8:
                    n2_back(0, tt - 1)
            if 'x1' in dbg_out:
                dump('x1', x1[:, :, :], [('x1', i) for i in range(NTT)])
            S.flush()
        sA.close()
        if stop_after == 'P3':
            sR.close()
            break

        with ExitStack() as st:
            gT = sb(st, "gT", [128, 22, 1024], BF)
            wd = sb(st, "wd", [128, 22, 512], BF)
            ARN = 3 * 2048 + 3 * 1028 + 4 * 1024 + 1024
            arena = sb(st, "arena", [128, ARN], BF)
            off = 0
            wup = []
            for i_ in range(3):
                wup.append(arena[:, off:off + 2048].rearrange("p (k c) -> p k c", k=8))
                off += 2048
            ybuf = []
            for i_ in range(3):
                ybuf.append(arena[:, off:off + 1028].bitcast(F32))
                off += 1028
            acc = []
            for i_ in range(4):
                acc.append(arena[:, off:off + 1024].bitcast(F32))
                off += 1024
            sa = arena[:, off:off + 1024].bitcast(F32)
            wd2 = arena[:, 0:22 * 512].rearrange("p (j c) -> p j c", j=22)
            ALIAS = ([('wup', i_, p_) for i_ in range(3) for p_ in range(2)] + [('yb', i_) for i_ in range(3)]
                     + [('acc', i_) for i_ in range(4)] + ['sa'])
            xo = [sb(st, f"xo{i}", [128, D], F32) for i in range(2)]
            ssF = sb(st, "ssF", [128, 16], F32)
            rsF = sb(st, "rsF", [128, 16], F32)
            if seq == 0:
                print("P5 sbuf bytes remaining", nc.sbuf_bytes_remaining, flush=True)

            MS('dve', ssF[:, 0:16], 0.0, [('ssF', i_) for i_ in range(16)])
            cnt = 0
            yi = 0
            ai = 0
            for hf in range(2):
                def ld_wup(jj):
                    wb_ = jj % 3
                    DMA('pool', wup[wb_][:, :, 0:128], w_up_v[:, :, 128 * jj:128 * jj + 128], (), [('wup', wb_, 0)])
                    DMA('pool', wup[wb_][:, :, 128:256], w_up_v[:, :, DFF + 128 * jj:DFF + 128 * jj + 128], (), [('wup', wb_, 1)])
                ld_wup(0)
                ld_wup(1)
                for j in range(22):
                    wb = j % 3
                    if j + 2 < 22:
                        ld_wup(j + 2)
                    if j == 2:
                        DMA('pool', wd[:, :, :], w_dn_v[:, :, 0:512], (), ['wd'])
                    for g2 in range(2):
                        first_group = (hf == 0 and g2 == 0)
                        last_group = (hf == 1 and g2 == 1)
                        accs = []
                        for part in range(2):
                            ct = part * 22 + j
                            bank = 2 + cnt % 4
                            cnt += 1
                            for kt in range(8):
                                MM(ps[bank][:, :], wup[wb][:, kt, part * 128:(part + 1) * 128],
                                   h2T[:, kt, g2 * 512:(g2 + 1) * 512], kt == 0, kt == 7,
                                   [('wup', wb, part)] + hkeys('h2T', g2 * 512, g2 * 512 + 512), [('ps', bank)])
                            yb = ybuf[yi % 3]
                            ykey = ('yb', yi % 3)
                            yi += 1
                            ACT(yb[:, 2:514], ps[bank][:, :], AF.Copy, [('ps', bank)], [ykey])
                            if first_group:
                                MS('pool', yb[:, 0:2], 0.0, [ykey])
                            else:
                                CP('pool', yb[:, 0:2], carry[:, ct, :], [('carry', ct)], [ykey])
                            if not last_group:
                                CP('pool', carry[:, ct, :], yb[:, 512:514], [ykey], [('carry', ct)])
                            a = acc[ai % 4]
                            akey = ('acc', ai % 4)
                            ai += 1
                            ACT(a[:, :], ps[bank][:, :], AF.Identity, [('ps', bank)], [akey],
                                scale=cw[:, 88 + ct:88 + ct + 1], bias=cb[:, ct:ct + 1])
                            STT('dve', a[:, :], yb[:, 1:513], cw[:, 44 + ct:44 + ct + 1], a[:, :], ALU.mult, ALU.add,
                                [ykey, akey], [akey])
                            STT('dve', a[:, :], yb[:, 0:512], cw[:, ct:ct + 1], a[:, :], ALU.mult, ALU.add,
                                [ykey, akey], [akey])
                            accs.append((a, akey))
                        ACT(sa[:, :], accs[0][0][:, :], AF.Silu, [accs[0][1]], ['sa'])
                        TT('dve', gT[:, j, g2 * 512:(g2 + 1) * 512], sa[:, :], accs[1][0][:, :], ALU.mult,
                           ['sa', accs[1][1]], [('gT', j, g2)])
                DMA('pool', wd2[:, :, :], w_dn_v[:, :, 512:1024], (), ['wd2'] + ALIAS)
                for cg in range(2):
                    csl = slice(cg * 512, (cg + 1) * 512)
                    wdt = wd if cg == 0 else wd2
                    wkeys = ['wd'] if cg == 0 else (['wd2'] + ALIAS)
                    for i in range(8):
                        tt = hf * 8 + i
                        bank = 6 + i % 2
                        if hf == 0 and cg == 0:
                            n2_front(1, i)
                        for j in range(22):
                            MM(ps[bank][:, :], gT[:, j, i * 128:(i + 1) * 128], wdt[:, j, :], j == 0, j == 21,
                               [('gT', j, i // 4)] + wkeys, [('ps', bank)])
                        TT('dve', x1[:, tt, csl], ps[bank][:, :], x1[:, tt, csl], ALU.add, [('ps', bank), ('x1', tt)], [('x1', tt)])
                        if hf == 0 and cg == 0:
                            n2_back(1, i)
                        if cg == 1:
                            c = hf * 8 + i
                            b = i % 2
                            tsl = slice(tt * 128, (tt + 1) * 128)
                            ACT(junk[:, :], x1[:, tt, :], AF.Square, [('x1', tt)], ['junk', ('ssF', c)], accum_out=ssF[:, c:c + 1])
                            ACT(rsF[:, c:c + 1], ssF[:, c:c + 1], AF.Ln, [('ssF', c)], [('rsF', c)], scale=1.0 / D, bias=EPS)
                            ACT(rsF[:, c:c + 1], rsF[:, c:c + 1], AF.Exp, [('rsF', c)], [('rsF', c)], scale=-0.5)
                            STT('dve', xo[b][:, :], x1[:, tt, :], rsF[:, c:c + 1], fing[:, :], ALU.mult, ALU.mult,
                                [('x1', tt), ('rsF', c)], [('xo', b)])
                            DMA('sp', out_d[seq, tsl, :], xo[b][:, :], [('xo', b)], ())
            S.flush()
        sR.close()

    top.close()
    return nc


_NC_CACHE = {}


def _prep_inputs(inputs):
    f = lambda a: np.ascontiguousarray(np.asarray(a, dtype=np.float32))
    g1 = f(inputs["attn_norm_g"])[0]
    g2 = f(inputs["ffn_norm_g"])[0]
    shared = {
        "w_in": f(inputs["w_in"])[0],
        "w_out": f(inputs["w_out"])[0],
        "w_up": f(inputs["w_ffn_up"])[0],
        "w_dn": f(inputs["w_ffn_down"])[0],
        "wgu": f(inputs["w_gate_up"])[0],
        "bgu": f(inputs["b_gate_up"])[0].reshape(1, 256),
        "gb1": np.ascontiguousarray(np.repeat(g1.reshape(8, 128).T[:, :, None], 128, axis=2).reshape(128, D)),
        "gb2": np.ascontiguousarray(np.repeat(g2.reshape(8, 128).T[:, :, None], 128, axis=2).reshape(128, D)),
        "fing": np.ascontiguousarray(np.broadcast_to(f(inputs["final_norm_g"])[None, :], (128, D))),
        "gsb": np.ascontiguousarray(f(inputs["sb_out_g"])[0].reshape(4, 128).T),
        "ggl": np.ascontiguousarray(f(inputs["gla_out_g"])[0].reshape(4, 128).T),
        "cw": np.ascontiguousarray(f(inputs["conv_w"])[0].reshape(3, 44, 128).transpose(2, 0, 1).reshape(128, 132)),
        "cb": np.ascontiguousarray(f(inputs["conv_b"])[0].reshape(44, 128).T),
    }
    for k, v in _consts().items():
        shared["c_" + k] = v
    return shared


def kernel(**inputs):
    x = np.ascontiguousarray(np.asarray(inputs["x"], dtype=np.float32))
    B = x.shape[0]
    nseq = B // NCORES
    shared = _prep_inputs(inputs)
    if nseq not in _NC_CACHE:
        _NC_CACHE[nseq] = build(nseq)
    nc = _NC_CACHE[nseq]
    in_maps = []
    for c in range(NCORES):
        m = dict(shared)
        m["x"] = x[c * nseq:(c + 1) * nseq]
        in_maps.append(m)
    res = run_bass_kernel_spmd(nc, in_maps, core_ids=list(range(NCORES)))
    return np.concatenate([r["out"] for r in res.results], axis=0)
```

```python
import numpy as np
from contextlib import ExitStack
import concourse.bass as bass
import concourse.mybir as mybir
from concourse.bass_utils import run_bass_kernel_spmd

F32 = mybir.dt.float32
BF = mybir.dt.bfloat16
AF = mybir.ActivationFunctionType
ALU = mybir.AluOpType

D = 1024
SEQ = 2048
NCORES = 8
DFF = 2816
EPS = 1e-6
NTT = SEQ // 128

ENGS = ('pe', 'act', 'dve', 'pool', 'sp')
EIDX = {e: i for i, e in enumerate(ENGS)}


class Sched:
    R = 4
    KD = 12

    def __init__(self, nc, es):
        self.nc = nc
        self.esem = {e: [es.enter_context(nc.semaphore(f"s_{e}{i}")) for i in range(self.R)]
                     for e in ENGS[:4]}
        self.dsem = {e: [es.enter_context(nc.semaphore(f"d_{e}{i}")) for i in range(self.KD)]
                     for e in ('sp', 'pool')}
        self.cnt = {e: 0 for e in ENGS}
        self.clock = {e: [0] * 5 for e in ENGS}
        self.snap = {e: [None] for e in ENGS}
        self.dcnt = {e: 0 for e in self.dsem}
        self.dseen = {e: {} for e in ENGS}
        self.lastw = {}
        self.readers = {}
        self.prog = {e: [] for e in ENGS}
        self.nwaits = 0
        self.nops = 0

    def _deps(self, eng, reads, writes):
        deps = set()
        for r in reads:
            ev = self.lastw.get(r)
            if ev is not None:
                deps.add(ev)
        for w in writes:
            ev = self.lastw.get(w)
            if ev is not None:
                deps.add(ev)
            for ev in self.readers.get(w, ()):
                deps.add(ev)
        best = {}
        for d in deps:
            if d[0] == 'c':
                _, e2, k = d
                if e2 == eng and eng == 'pe':
                    continue
                if self.clock[eng][EIDX[e2]] >= k:
                    continue
                if best.get(e2, 0) < k:
                    best[e2] = k
            else:
                _, pe_, idx, val = d
                if self.dseen[eng].get((pe_, idx), 0) >= val:
                    continue
                self._wait_dma(eng, pe_, idx, val)
        for e2, k in best.items():
            if self.clock[eng][EIDX[e2]] >= k:
                continue
            self._wait_eng(eng, e2, k)

    def _semval(self, k):
        r = k % self.R
        first = r if r != 0 else self.R
        return (k - first) // self.R + 1

    def _wait_eng(self, eng, e2, k):
        sem = self.esem[e2][k % self.R]
        self.prog[eng].append(('w', sem, self._semval(k)))
        self.nwaits += 1
        sn = self.snap[e2][k]
        c = self.clock[eng]
        for i in range(5):
            if sn[i] > c[i]:
                c[i] = sn[i]

    def _wait_dma(self, eng, pe_, idx, val):
        self.prog[eng].append(('w', self.dsem[pe_][idx], val))
        self.nwaits += 1
        self.dseen[eng][(pe_, idx)] = val

    def _record(self, ev, reads, writes):
        for w in writes:
            self.lastw[w] = ev
            self.readers[w] = []
        for r in reads:
            lst = self.readers.setdefault(r, [])
            if ev[0] == 'c':
                lst[:] = [x for x in lst if not (x[0] == 'c' and x[1] == ev[1])]
            lst.append(ev)

    def op(self, eng, fn, reads=(), writes=()):
        self._deps(eng, reads, writes)
        self.cnt[eng] += 1
        k = self.cnt[eng]
        sn = list(self.clock[eng])
        sn[EIDX[eng]] = k
        self.snap[eng].append(tuple(sn))
        self.prog[eng].append(('o', fn, self.esem[eng][k % self.R], 1))
        self.nops += 1
        self._record(('c', eng, k), reads, writes)

    def dma(self, eng, fn, reads=(), writes=()):
        self._deps(eng, reads, writes)
        d = self.dcnt[eng]
        self.dcnt[eng] += 1
        idx = d % self.KD
        prev = 16 * (d // self.KD)
        if prev > 0 and self.dseen[eng].get((eng, idx), 0) < prev:
            self._wait_dma(eng, eng, idx, prev)
        self.prog[eng].append(('o', fn, self.dsem[eng][idx], 16))
        self.nops += 1
        self._record(('d', eng, idx, prev + 16), reads, writes)

    def _final_vals(self):
        out = {}
        for pe_ in self.dsem:
            n = self.dcnt[pe_]
            for idx in range(min(n, self.KD)):
                out[(pe_, idx)] = 16 * ((n - idx + self.KD - 1) // self.KD)
        return out

    def flush(self):
        nc = self.nc
        fv = self._final_vals()
        for (pe_, idx), val in fv.items():
            if self.dseen['sp'].get((pe_, idx), 0) < val:
                self._wait_dma('sp', pe_, idx, val)
        prog = self.prog

        def mk(name):
            items = prog[name]

            def f(e):
                for it in items:
                    if it[0] == 'w':
                        e.wait_ge(it[1], it[2])
                    else:
                        it[1](e).then_inc(it[2], it[3])
            return f

        with nc.Block(no_gpsimd_drain=True) as block:
            block.tensor(mk('pe'))
            block.scalar(mk('act'))
            block.vector(mk('dve'))
            block.gpsimd(mk('pool'))
            block.sync(mk('sp'))
        self.prog = {e: [] for e in ENGS}
        full = [self.cnt[e] for e in ENGS]
        for e in ENGS:
            self.clock[e] = list(full)
            self.dseen[e] = dict(fv)
        self.lastw = {}
        self.readers = {}


def _consts():
    i = np.arange(128)
    a, b = i[:, None], i[None, :]
    same = (a // 64) == (b // 64)
    c = {}
    c["ident"] = np.eye(128, dtype=np.float32)
    c["triinc_neg"] = -(a >= b).astype(np.float32)
    c["mask_strict"] = (a < b).astype(np.float32)
    gc = -1.0 / 16.0
    c["gl1"] = (gc * ((a > b) & same)).astype(np.float32)
    r2 = np.zeros((128, 130), np.float32)
    r2[:, :128] = gc * ((a <= b) & same)
    r2[:, 128] = gc * (i < 64)
    r2[:, 129] = gc * (i >= 64)
    c["gr2"] = r2
    c["mask_gla"] = ((a <= b) & same).astype(np.float32)
    c["bdones"] = same.astype(np.float32)
    return c


CONST_SHAPES = {"ident": (128, 128), "triinc_neg": (128, 128), "mask_strict": (128, 128),
                "gl1": (128, 128), "gr2": (128, 130), "mask_gla": (128, 128), "bdones": (128, 128)}


class StopBuild(Exception):
    pass


def build(nseq, stop_after=None, dbg=()):
    try:
        return _build(nseq, stop_after, dbg)
    except StopBuild as e:
        return e.args[0]


def _build(nseq, stop_after=None, dbg=()):
    nc = bass.Bass("TRN2", target_bir_lowering=False)
    import os as _os
    _stopk = int(_os.environ.get("STOPK", "0"))
    _ckn = [0]
    dbg = dict(dbg)
    dbg_out = {}

    def din(name, shape, dt=F32):
        return nc.dram_tensor(name, list(shape), dt, kind="ExternalInput").ap()

    x_d = din("x", [nseq, SEQ, D])
    w_in_d = din("w_in", [D, 3088])
    w_out_d = din("w_out", [D, D])
    w_up_d = din("w_up", [D, 2 * DFF])
    w_dn_d = din("w_dn", [DFF, D])
    wgu_d = din("wgu", [16, 256])
    bgu_d = din("bgu", [1, 256])
    gb1_d = din("gb1", [128, D])
    gb2_d = din("gb2", [128, D])
    fing_d = din("fing", [128, D])
    gsb_d = din("gsb", [128, 4])
    ggl_d = din("ggl", [128, 4])
    cw_d = din("cw", [128, 3 * 44])
    cb_d = din("cb", [128, 44])
    cst_d = {k: din("c_" + k, v) for k, v in CONST_SHAPES.items()}
    out_d = nc.dram_tensor("out", [nseq, SEQ, D], F32, kind="ExternalOutput").ap()
    for name, shape in dbg.items():
        dbg_out[name] = nc.dram_tensor("dbg_" + name, list(shape), F32, kind="ExternalOutput").ap()

    w_in_v = w_in_d.rearrange("(kt p) c -> p kt c", p=128)
    w_up_v = w_up_d.rearrange("(kt p) c -> p kt c", p=128)
    w_dn_v = w_dn_d.rearrange("(j p) c -> p j c", p=128)
    w_out_sb_v = w_out_d[0:512, :].rearrange("(h p) c -> p h c", p=128)
    w_out_gl_v = w_out_d[512:1024, :].rearrange("(h p) c -> p h c", p=128)

    top = ExitStack()
    S = Sched(nc, top)
    manual = []

    def ck(tag=""):
        _ckn[0] += 1
        if _stopk and _ckn[0] == _stopk:
            print("STOP at checkpoint", _ckn[0], tag, flush=True)
            S.flush()
            raise StopBuild(nc)

    uid = [0]

    def sb(stack, name, shape, dt, side=None):
        uid[0] += 1
        name = f"t{uid[0]}_{name}"
        if side is None:
            return stack.enter_context(nc.sbuf_tensor(name, list(shape), dt))
        return stack.enter_context(nc.sbuf_tensor(name, list(shape), dt, side=side))

    def ACT(out, in_, func, r, w, **kw):
        S.op('act', lambda e: e.activation(out=out, in_=in_, func=func, **kw), r, w)

    def TS(eng, out, in0, s1, s2, op0, op1, r, w):
        if s2 is None:
            S.op(eng, lambda e: e.tensor_scalar(out=out, in0=in0, scalar1=s1, scalar2=None, op0=op0), r, w)
        else:
            S.op(eng, lambda e: e.tensor_scalar(out=out, in0=in0, scalar1=s1, scalar2=s2, op0=op0, op1=op1), r, w)

    def TT(eng, out, in0, in1, op, r, w):
        S.op(eng, lambda e: e.tensor_tensor(out=out, in0=in0, in1=in1, op=op), r, w)

    def STT(eng, out, in0, scalar, in1, op0, op1, r, w):
        S.op(eng, lambda e: e.scalar_tensor_tensor(out=out, in0=in0, scalar=scalar, in1=in1, op0=op0, op1=op1), r, w)

    def CP(eng, out, in_, r, w):
        S.op(eng, lambda e: e.tensor_copy(out=out, in_=in_), r, w)

    def MS(eng, ap, val, w):
        S.op(eng, lambda e: e.memset(ap, val), (), w)

    def MM(out, lhsT, rhs, start, stop, r, w, sgc=False):
        if sgc:
            S.op('pe', lambda e: e.matmul(out, lhsT, rhs, start=start, stop=stop, skip_group_check=True), r, w)
        else:
            S.op('pe', lambda e: e.matmul(out, lhsT, rhs, start=start, stop=stop), r, w)

    def TR(out, in_, ident, r, w):
        S.op('pe', lambda e: e.transpose(out, in_, ident), r, w)

    def DMA(eng, out, in_, r, w):
        S.dma(eng, lambda e: e.dma_start(out=out, in_=in_), r, w)

    def dump(name, ap, keys, rows=None):
        if name in dbg_out:
            DMA('sp', dbg_out[name] if rows is None else dbg_out[name][rows], ap, keys, ())

    ps = [top.enter_context(nc.psum_tensor(f"ps{i}", [128, 512], F32)) for i in range(8)]
    psb = [ps[i][:, :].bitcast(BF) for i in range(2)]

    ident = sb(top, "ident", [128, 128], BF)
    triinc = sb(top, "triinc", [128, 128], BF)
    onesneg = sb(top, "onesneg", [128, 128], BF)
    mstrict = sb(top, "mstrict", [128, 128], BF)
    gl1 = sb(top, "gl1", [128, 128], F32)
    gr2 = sb(top, "gr2", [128, 130], F32)
    mgla = sb(top, "mgla", [128, 128], F32)
    onesf = sb(top, "onesf", [128, 128], F32)
    onesrow = sb(top, "onesrow", [1, 128], BF)
    bgu = sb(top, "bgu", [1, 256], BF)
    wgu = sb(top, "wgu", [16, 256], BF)
    gb1 = sb(top, "gb1", [128, D], F32)
    gb2 = sb(top, "gb2", [128, D], F32)
    fing = sb(top, "fing", [128, D], F32)
    gsb = sb(top, "gsb", [128, 4], F32)
    bdones = sb(top, "bdones", [128, 128], BF)
    onesb = sb(top, "onesb", [128, 128], BF)
    ggl = sb(top, "ggl", [128, 4], F32)
    cw = sb(top, "cw", [128, 3 * 44], F32)
    cb = sb(top, "cb", [128, 44], F32)
    ss = sb(top, "ss", [128, 16], F32)
    rs = sb(top, "rs", [128, 16], F32)

    DMA('pool', ident[:, :], cst_d["ident"], (), ['ident'])
    DMA('pool', triinc[:, :], cst_d["triinc_neg"], (), ['triinc'])
    DMA('pool', mstrict[:, :], cst_d["mask_strict"], (), ['mstrict'])
    DMA('pool', bgu[:, :], bgu_d, (), ['bgu'])
    DMA('pool', wgu[:, :], wgu_d, (), ['wgu'])
    DMA('sp', gl1[:, :], cst_d["gl1"], (), ['gl1'])
    DMA('sp', gr2[:, :], cst_d["gr2"], (), ['gr2'])
    DMA('sp', mgla[:, :], cst_d["mask_gla"], (), ['mgla'])
    DMA('sp', gb1[:, :], gb1_d, (), ['gb1'])
    DMA('sp', gb2[:, :], gb2_d, (), ['gb2'])
    DMA('sp', fing[:, :], fing_d, (), ['fing'])
    DMA('sp', gsb[:, :], gsb_d, (), ['gsb'])
    DMA('pool', bdones[:, :], cst_d["bdones"], (), ['bdones'])
    MS('dve', onesb[:, :], 1.0, ['onesb'])
    DMA('sp', ggl[:, :], ggl_d, (), ['ggl'])
    DMA('sp', cw[:, :], cw_d, (), ['cw'])
    DMA('sp', cb[:, :], cb_d, (), ['cb'])
    MS('dve', onesneg[:, :], -1.0, ['onesneg'])
    MS('dve', onesf[:, :], 1.0, ['onesf'])
    MS('dve', onesrow[:, :], 1.0, ['onesrow'])
    S.flush()

    def norm_T(st, srcs, gbt, dstT, dkey):
        xn = [sb(st, f"xn{i}", [128, D], BF) for i in range(2)]
        junk = sb(st, "junk", [128, D], BF)
        n = len(srcs)
        MS('dve', ss[:, 0:n], 0.0, ['ss'])
        for i, (src, skeys) in enumerate(srcs):
            b = i % 2
            ACT(junk[:, :], src, AF.Square, skeys, ['junk', 'ss'], accum_out=ss[:, i:i + 1])
            ACT(rs[:, i:i + 1], ss[:, i:i + 1], AF.Ln, ['ss'], [('rs', i)], scale=1.0 / D, bias=EPS)
            ACT(rs[:, i:i + 1], rs[:, i:i + 1], AF.Exp, [('rs', i)], [('rs', i)], scale=-0.5)
            TS('dve', xn[b][:, :], src, rs[:, i:i + 1], None, ALU.mult, None, list(skeys) + [('rs', i)], [('xn', b)])
            for kt in range(8):
                TR(psb[b][:, kt * 128:(kt + 1) * 128], xn[b][:, kt * 128:(kt + 1) * 128], ident[:, :],
                   [('xn', b), 'ident'], [('ps', b)])
            TT('dve', dstT[:, :, i * 128:(i + 1) * 128],
               psb[b].rearrange("p (k t) -> p k t", k=8),
               gbt[:, :].rearrange("p (k t) -> p k t", k=8), ALU.mult,
               [('ps', b), 'gbt'], [(dkey, i)])

    def hkeys(key, c0, c1):
        return [(key, i) for i in range(c0 // 128, (c1 + 127) // 128)]

    pj = [0]

    def proj_fm(W, wkey, wc0, M, srcT, skey, ntok, evac):
        for tg in range(ntok // 512):
            bank = 2 + (pj[0] % 2)
            pj[0] += 1
            for kt in range(8):
                MM(ps[bank][0:M, :], W[:, kt, wc0:wc0 + M], srcT[:, kt, tg * 512:(tg + 1) * 512],
                   kt == 0, kt == 7, [wkey] + hkeys(skey, tg * 512, tg * 512 + 512), [('ps', bank)])
            evac(tg, ps[bank][0:M, :], [('ps', bank)])

    carry = sb(top, "carry", [128, 44, 2], F32)
    SS16 = [('ss', i) for i in range(16)]

    for seq in range(nseq):
        sA = ExitStack()
        osT = sb(sA, "osT", [128, 4, SEQ], BF)
        ogT = sb(sA, "ogT", [128, 4, SEQ], BF)
        sB = ExitStack()
        hT = sb(sB, "hT", [128, 8, SEQ], BF)

        with ExitStack() as st:
            wk = sb(st, "wk", [128, 8, 256], BF)
            wq = sb(st, "wq", [128, 8, 256], BF)
            wog = sb(st, "wog", [128, 8, 512], BF)
            vtok = sb(st, "vtok", [128, 16, 512], BF)
            kend = sb(st, "kend", [128, 16, 256], BF)
            sptok = sb(st, "sptok", [128, 16, 256], F32)
            with ExitStack() as stA_:
                wv = sb(stA_, "wv", [128, 8, 512], BF)
                wlr = sb(stA_, "wlr", [128, 8, 16], BF)
                glrT = sb(stA_, "glrT", [16, SEQ], BF)
                e1 = [sb(stA_, f"e1{i}", [128, 256], F32) for i in range(2)]
                erb = [sb(stA_, f"erb{i}", [128, 256], F32) for i in range(2)]
                DMA('pool', wlr[:, :, :], w_in_v[:, :, 2560:2576], (), ['wlr'])
                DMA('pool', wk[:, :, :], w_in_v[:, :, 1792:2048], (), ['wk'])
                DMA('pool', wv[:, :, :], w_in_v[:, :, 2048:2560], (), ['wv'])
                DMA('pool', wq[:, :, :], w_in_v[:, :, 1536:1792], (), ['wq'])
                DMA('pool', wog[:, :, :], w_in_v[:, :, 2576:3088], (), ['wog'])
                xt = [sb(stA_, f"xt{i}", [128, D], F32) for i in range(4)]
                xn = [sb(stA_, f"xn{i}", [128, D], BF) for i in range(2)]
                junk = sb(stA_, "junk", [128, D], BF)
                MS('dve', ss[:, 0:16], 0.0, SS16)

                def p1_tile(i):
                    b = i % 2
                    xb = i % 4
                    DMA('sp', xt[xb][:, :], x_d[seq, i * 128:(i + 1) * 128, :], (), [('xt', xb)])
                    src = xt[xb][:, :]
                    skeys = [('xt', xb)]
                    ACT(junk[:, :], src, AF.Square, skeys, ['junk', ('ss', i)], accum_out=ss[:, i:i + 1])
                    ACT(rs[:, i:i + 1], ss[:, i:i + 1], AF.Ln, [('ss', i)], [('rs', i)], scale=1.0 / D, bias=EPS)
                    ACT(rs[:, i:i + 1], rs[:, i:i + 1], AF.Exp, [('rs', i)], [('rs', i)], scale=-0.5)
                    TS('dve', xn[b][:, :], src, rs[:, i:i + 1], None, ALU.mult, None, skeys + [('rs', i)], [('xn', b)])
                    for kt in range(8):
                        TR(psb[b][:, kt * 128:(kt + 1) * 128], xn[b][:, kt * 128:(kt + 1) * 128], ident[:, :],
                           [('xn', b)], [('ps', b)])
                    TT('dve', hT[:, :, i * 128:(i + 1) * 128],
                       psb[b].rearrange("p (k t) -> p k t", k=8),
                       gb1[:, :].rearrange("p (k t) -> p k t", k=8), ALU.mult,
                       [('ps', b)], [('hT', i)])

                def glr_group(tg):
                    bank = 2 + (pj[0] % 2)
                    pj[0] += 1
                    for kt in range(8):
                        MM(ps[bank][0:16, :], wlr[:, kt, 0:16], hT[:, kt, tg * 512:(tg + 1) * 512],
                           kt == 0, kt == 7, ['wlr'] + hkeys('hT', tg * 512, tg * 512 + 512), [('ps', bank)])
                    ACT(glrT[0:16, tg * 512:(tg + 1) * 512], ps[bank][0:16, :], AF.Copy, [('ps', bank)], [('glrT', tg)])

                def glaA_tile(tt):
                    b = tt % 2
                    tsl = slice(tt * 128, (tt + 1) * 128)
                    bp, brb = (4, 5) if b == 0 else (2, 3)
                    bk_, bv_ = (6, 7)
                    MM(ps[bp][:, 0:256], glrT[0:16, tsl], wgu[0:16, :], True, False, [('glrT', tt // 4)], [('ps', bp)])
                    MM(ps[bp][:, 0:256], onesrow[0:1, :], bgu[0:1, :], False, True, [], [('ps', bp)])
                    ACT(e1[b][:, :], ps[bp][:, 0:256], AF.Exp, [('ps', bp)], [('e1', b)], scale=-1.0)
                    ACT(sptok[:, tt, :], e1[b][:, :], AF.Ln, [('e1', b)], [('sp', tt)], bias=1.0)
                    MM(ps[brb][:, 0:256], gl1[:, :], sptok[:, tt, :], True, True, [('sp', tt)], [('ps', brb)])
                    ACT(erb[b][:, :], ps[brb][:, 0:256], AF.Exp, [('ps', brb)], [('erb', b)])
                    for kt in range(8):
                        MM(ps[bk_][:, 0:256], hT[:, kt, tsl], wk[:, kt, :], kt == 0, kt == 7, [('hT', tt), 'wk'], [('ps', bk_)])
                    TT('dve', kend[:, tt, :], ps[bk_][:, 0:256], erb[b][:, :], ALU.mult, [('ps', bk_), ('erb', b)], [('kend', tt)])
                    for kt in range(8):
                        MM(ps[bv_][:, :], hT[:, kt, tsl], wv[:, kt, :], kt == 0, kt == 7, [('hT', tt), 'wv'], [('ps', bv_)])
                    CP('dve', vtok[:, tt, :], ps[bv_][:, :], [('ps', bv_)], [('vtok', tt)])

                for i in range(NTT + 5):
                    if i < NTT:
                        p1_tile(i)
                    if 1 <= i <= NTT and i % 4 == 0:
                        glr_group(i // 4 - 1)
                    if i >= 5:
                        glaA_tile(i - 5)
                S.flush()

            gq = [sb(st, f"gq{i}", [64, SEQ], BF) for i in range(2)]
            gk = [sb(st, f"gk{i}", [64, SEQ], BF) for i in range(2)]
            sog = [sb(st, f"sog{i}", [128, SEQ], BF) for i in range(2)]
            Sall = [sb(st, f"Sall{i}", [64, 32, 128], BF) for i in range(2)]
            Sf = [[sb(st, f"Sf{i}{j}", [64, 128], F32) for j in range(2)] for i in range(2)]
            dec = [sb(st, f"dec{i}", [64, 32], F32) for i in range(2)]
            eb = [[sb(st, f"eb{i}{j}", [64, 128], F32) for j in range(2)] for i in range(2)]
            enb = [[sb(st, f"enb{i}{j}", [64, 128], F32) for j in range(2)] for i in range(2)]
            attn = [[sb(st, f"attn{i}{j}", [128, 128], BF) for j in range(2)] for i in range(2)]
            osq = [sb(st, f"osq{i}", [128, 512], BF) for i in range(2)]
            of = [sb(st, f"of{i}", [128, 512], F32) for i in range(2)]
            rst = [sb(st, f"rst{i}", [128, 512], F32) for i in range(2)]
            if seq == 0:
                print("GLA sbuf bytes remaining", nc.sbuf_bytes_remaining, flush=True)

            def proj_group(W, wkey, wc0, M, tg, evac):
                bank = 2 + (pj[0] % 2)
                pj[0] += 1
                for kt in range(8):
                    MM(ps[bank][0:M, :], W[:, kt, wc0:wc0 + M], hT[:, kt, tg * 512:(tg + 1) * 512],
                       kt == 0, kt == 7, [wkey], [('ps', bank)])
                evac(tg, ps[bank][0:M, :], [('ps', bank)])

            for pp in range(2):
                def bT_steps(tg_):
                    for tt in range(4 * tg_, 4 * tg_ + 4):
                        b = tt % 2
                        tg = tt // 4
                        tsl = slice(tt * 128, (tt + 1) * 128)
                        for q in range(2):
                            h = 2 * pp + q
                            bb = 4 + q
                            MM(ps[bb][0:64, 0:130], sptok[:, tt, 64 * h:64 * h + 64], gr2[:, :], True, True, [('sp', tt)], [('ps', bb)])
                            ACT(eb[q][b][:, :], ps[bb][0:64, 0:128], AF.Exp, [('ps', bb)], [('eb', q, b)])
                            ACT(enb[q][b][:, :], ps[bb][0:64, 0:128], AF.Exp, [('ps', bb)], [('enb', q, b)], scale=-1.0)
                            ACT(dec[q][:, 2 * tt:2 * tt + 2], ps[bb][0:64, 128:130], AF.Exp, [('ps', bb)], [('dec', q, tt)])
                            TT('dve', gq[q][0:64, tsl], gq[q][0:64, tsl], eb[q][b][:, :], ALU.mult, [('gq', q, tg), ('eb', q, b)], [('gq', q, tg)])
                            TT('dve', gk[q][0:64, tsl], gk[q][0:64, tsl], enb[q][b][:, :], ALU.mult, [('gk', q, tg), ('enb', q, b)], [('gk', q, tg)])

                for tg_ in range(4):
                    for q in range(2):
                        h = 2 * pp + q

                        def ev_q(tg, p, pk, q=q):
                            ACT(gq[q][0:64, tg * 512:(tg + 1) * 512], p, AF.Copy, pk, [('gq', q, tg)], scale=0.125)

                        def ev_k(tg, p, pk, q=q):
                            ACT(gk[q][0:64, tg * 512:(tg + 1) * 512], p, AF.Copy, pk, [('gk', q, tg)])
                        proj_group(wq, 'wq', 64 * h, 64, tg_, ev_q)
                        proj_group(wk, 'wk', 64 * h, 64, tg_, ev_k)
                    if tg_ >= 1:
                        bT_steps(tg_ - 1)
                bT_steps(3)
                for q in range(2):
                    h = 2 * pp + q

                    def ev_og(tg, p, pk, q=q):
                        ACT(sog[q][:, tg * 512:(tg + 1) * 512], p, AF.Silu, pk, [('sog', q, tg)])
                    proj_fm(wog, 'wog', 128 * h, 128, hT, 'hT', SEQ, ev_og)
                for q in range(2):
                    MS('dve', Sf[q][0][:, :], 0.0, [('Sf', q, 0)])
                    MS('dve', Sall[q][:, 0, :], 0.0, [('Sall', q, 0)])
                for n in range(32):
                    tt, half = n // 2, n % 2
                    pr = slice(64 * half, 64 * half + 64)
                    for q in range(2):
                        h = 2 * pp + q
                        kbank = ((2, 3), (6, 7))[q][n % 2]
                        kvp = ps[kbank][0:64, 0:128]
                        MM(kvp, kend[pr, tt, 64 * h:64 * h + 64], vtok[pr, tt, 128 * h:128 * h + 128], True, True,
                           [('kend', tt), ('vtok', tt)], [('ps', kbank)])
                        cur, prev = Sf[q][(n + 1) % 2], Sf[q][n % 2]
                        STT('dve', cur[:, :], prev[:, :], dec[q][:, n:n + 1], kvp, ALU.mult, ALU.add,
                            [('Sf', q, n % 2), ('dec', q, tt), ('ps', kbank)], [('Sf', q, (n + 1) % 2)])
                        if n < 31:
                            ACT(Sall[q][:, n + 1, :], cur[:, :], AF.Copy, [('Sf', q, (n + 1) % 2)], [('Sall', q, n + 1)])
                gdef = []

                def g_run():
                    for d_ in list(gdef):
                        d_[0] -= 1
                    for d_ in list(gdef):
                        if d_[0] <= 0:
                            d_[1]()
                            gdef.remove(d_)

                for tg in range(4):
                    gsl = slice(tg * 512, (tg + 1) * 512)
                    for qq in range(4):
                        tt = tg * 4 + qq
                        b = tt % 2
                        tsl = slice(tt * 128, (tt + 1) * 128)
                        for q in range(2):
                            h = 2 * pp + q
                            ab = ps[6 + q]
                            abk = ('ps', 6 + q)
                            obn = q if tg % 2 == 0 else 4 + q
                            ob = ps[obn]
                            obk = ('ps', obn)
                            MM(ab[:, 0:128], gk[q][0:64, tsl], gq[q][0:64, tsl], True, True, [('gk', q, tg), ('gq', q, tg)], [abk])
                            TT('dve', attn[q][b][:, :], ab[:, 0:128], mgla[:, :], ALU.mult, [abk], [('attn', q, b)])
                            MM(ob[:, qq * 128:(qq + 1) * 128], vtok[:, tt, 128 * h:128 * h + 128], attn[q][b][:, :], qq == 0, False,
                               [('vtok', tt), ('attn', q, b)], [obk], sgc=True)
                            MM(ob[:, qq * 128:qq * 128 + 64], Sall[q][:, 2 * tt, :], gq[q][0:64, tt * 128:tt * 128 + 64], False, False,
                               [('Sall', q, 2 * tt), ('gq', q, tg)], [obk], sgc=True)
                            MM(ob[:, qq * 128 + 64:qq * 128 + 128], Sall[q][:, 2 * tt + 1, :], gq[q][0:64, tt * 128 + 64:tt * 128 + 128], False, True,
                               [('Sall', q, 2 * tt + 1), ('gq', q, tg)], [obk], sgc=True)
                        g_run()
                    for q in range(2):
                        def mkhn(q=q, tg=tg, gsl=gsl):
                            h = 2 * pp + q
                            obn = q if tg % 2 == 0 else 4 + q
                            ob = ps[obn]
                            obk = ('ps', obn)

                            def h1():
                                CP('dve', of[q][:, :], ob[:, :], [obk], [('of', q)])
                                ACT(osq[q][:, :], of[q][:, :], AF.Square, [('of', q)], [('osq', q)])

                            def h2():
                                MM(ps[2 + q][:, :], onesb[:, :], osq[q][:, :], True, True, [('osq', q)], [('ps', 2 + q)])

                            def h3():
                                ACT(rst[q][:, :], ps[2 + q][:, :], AF.Ln, [('ps', 2 + q)], [('rst', q)], scale=1.0 / 128, bias=EPS)
                                ACT(rst[q][:, :], rst[q][:, :], AF.Exp, [('rst', q)], [('rst', q)], scale=-0.5)

                            def h4():
                                STT('dve', of[q][:, :], of[q][:, :], ggl[:, h:h + 1], rst[q][:, :], ALU.mult, ALU.mult, [('of', q), ('rst', q)], [('of', q)])
                                TT('dve', ogT[:, h, gsl], of[q][:, :], sog[q][:, gsl], ALU.mult, [('of', q), ('sog', q, tg)], [('ogT', h, tg)])
                            h1()
                            gdef.append([1, h2])
                            gdef.append([2, h3])
                            gdef.append([3, h4])
                        mkhn()
                while gdef:
                    g_run()
            S.flush()
            if 'ogT' in dbg_out:
                DMA('pool', dbg_out['ogT'], ogT[:, :, :], [], ())
                S.flush()
        if stop_after == 'GLA':
            sB.close(); sA.close()
            break

        with ExitStack() as st:
            wq = sb(st, "wsq", [128, 8, 512], BF)
            wk = sb(st, "wsk", [128, 8, 512], BF)
            wv = sb(st, "wsv", [128, 8, 512], BF)
            DMA('pool', wv[:, :, :], w_in_v[:, :, 1024:1536], (), ['wsv'])
            DMA('pool', wq[:, :, :], w_in_v[:, :, 0:512], (), ['wsq'])
            DMA('pool', wk[:, :, :], w_in_v[:, :, 512:1024], (), ['wsk'])
            vtok = sb(st, "vts", [128, 16, 512], BF)
            sqp = [sb(st, f"sq{i}", [128, SEQ], BF) for i in range(2)]
            skp = [[sb(st, f"sk{i}{j}", [128, SEQ], BF) for j in range(2)] for i in range(2)]
            for i_ in range(2):
                MS('pool', skp[i_][0][64:128, :], 0.0, [('skz', i_, 0)])
                MS('pool', skp[i_][1][0:64, :], 0.0, [('skz', i_, 1)])
            esb = [sb(st, f"esb{i}", [128, 512], BF) for i in range(3)]
            spb = [sb(st, f"spb{i}", [128, 512], BF) for i in range(2)]
            wsb = [sb(st, f"wsb{i}", [128, 512], BF) for i in range(2)]
            Ls = [sb(st, f"Ls{i}", [128, 512], BF) for i in range(2)]
            osq = sb(st, "osq", [128, 512], BF)
            of = sb(st, "of", [128, 512], F32)
            rst = sb(st, "rst", [128, 512], F32)
            MS('dve', osq[:, :], 0.0, ['osq'])
            def v_chunk(tt):
                bank = 2 + (pj[0] % 2)
                pj[0] += 1
                tsl = slice(tt * 128, (tt + 1) * 128)
                for kt in range(8):
                    MM(ps[bank][:, :], hT[:, kt, tsl], wv[:, kt, :], kt == 0, kt == 7, [('hT', tt), 'wsv'], [('ps', bank)])
                CP('dve', vtok[:, tt, :], ps[bank][:, :], [('ps', bank)], [('vts', tt)])
            for tt in range(4):
                v_chunk(tt)
            tiles = [(h, G, kb) for h in range(8) for G in range(4) for kb in range(4 * G + 3, -1, -1)]
            NT = len(tiles)
            spb3 = spb + [sb(st, "spb2", [128, 512], BF)]
            wsb3 = wsb + [sb(st, "wsb2", [128, 512], BF)]

            pending = []

            def sb_proj_chunks(P):
                pp = P % 2
                out = []
                for which in ('q', 'k'):
                    W = wq if which == 'q' else wk
                    for tg in range(4):
                        def mk(W=W, which=which, tg=tg):
                            state = {}
                            tsl_ = slice(tg * 512, (tg + 1) * 512)

                            def c1():
                                bank = 2 + (pj[0] % 2)
                                pj[0] += 1
                                state['bank'] = bank
                                for kt in range(4):
                                    MM(ps[bank][:, :], W[:, kt, 128 * P:128 * P + 128], hT[:, kt, tsl_],
                                       kt == 0, False, ['wsq' if which == 'q' else 'wsk'], [('ps', bank)])

                            def c2():
                                bank = state['bank']
                                for kt in range(4, 8):
                                    MM(ps[bank][:, :], W[:, kt, 128 * P:128 * P + 128], hT[:, kt, tsl_],
                                       False, kt == 7, ['wsq' if which == 'q' else 'wsk'], [('ps', bank)])
                                if which == 'q':
                                    TS('dve', sqp[pp][:, tsl_], ps[bank][:, :], 0.125, None, ALU.mult, None,
                                       [('ps', bank)], [('sq', pp, tg)])
                                else:
                                    CP('dve', skp[pp][0][0:64, tsl_], ps[bank][0:64, :], [('ps', bank)], [('sk', pp, 0, tg)])
                                    CP('dve', skp[pp][1][64:128, tsl_], ps[bank][64:128, :], [('ps', bank)], [('sk', pp, 1, tg)])
                            return [c1, c2]
                        out.extend(mk())
                return out

            def tinfo(i):
                h, G, kb = tiles[i]
                j = kb - 4 * G
                c0 = 128 * j if j > 0 else 0
                pp = (h // 2) % 2
                return dict(h=h, pp=pp, G=G, kb=kb, c0=c0, diag=(j >= 0), first=(kb == 4 * G + 3), last=(kb == 0),
                            t0=512 * G + c0, t1=512 * G + 512, ksl=slice(kb * 128, (kb + 1) * 128),
                            sq=sqp[pp], sk=skp[pp][h % 2],
                            kkey=('sk', pp, h % 2, kb // 4), qkey=('sq', pp, G), zkey=('skz', pp, h % 2),
                            lk=('L', G % 2), L=Ls[G % 2],
                            obk=('ps', 4 + G % 2), ob=ps[4 + G % 2])

            ZB = (0, 1, 6)

            def stA(i):
                t = tinfo(i)
                c0 = t['c0']
                if t['first']:
                    if t['G'] == 0 and t['h'] % 2 == 0:
                        while pending:
                            pending.pop(0)()
                        if t['h'] // 2 + 1 < 4:
                            pending.extend(sb_proj_chunks(t['h'] // 2 + 1))
                    MS('pool', t['L'][:, :], 0.0, [t['lk']])
                elif pending and not t['diag']:
                    pending.pop(0)()
                zbn = ZB[i % 3]
                zb = ps[zbn]
                MM(zb[:, c0:512], t['sk'][:, t['ksl']], t['sq'][:, t['t0']:t['t1']], True, False,
                   [t['kkey'], t['qkey'], t['zkey']], [('ps', zbn)], sgc=True)
                ACT(esb[i % 3][:, c0:512], zb[:, c0:512], AF.Exp, [('ps', zbn)], [('esb', i % 3)])
                ACT(spb3[i % 3][:, c0:512], esb[i % 3][:, c0:512], AF.Ln, [('esb', i % 3)], [('spb', i % 3)], bias=1.0)
                if t['diag']:
                    TT('dve', spb3[i % 3][:, c0:c0 + 128], spb3[i % 3][:, c0:c0 + 128], mstrict[:, :], ALU.mult,
                       [('spb', i % 3)], [('spb', i % 3)])

            def stB(i):
                t = tinfo(i)
                c0 = t['c0']
                zbn = ZB[i % 3]
                ab = ps[zbn]
                abk = ('ps', zbn)
                MM(ab[:, c0:512], triinc[:, :], spb3[i % 3][:, c0:512], False, t['first'], [('spb', i % 3)], [abk], sgc=True)
                if not t['first']:
                    MM(ab[:, c0:512], onesneg[:, :], t['L'][:, c0:512], False, True, [t['lk']], [abk], sgc=True)
                ACT(wsb3[i % 3][:, c0:512], ab[:, c0:512], AF.Exp, [abk], [('wsb', i % 3)])
                if t['diag']:
                    TT('dve', wsb3[i % 3][:, c0:c0 + 128], wsb3[i % 3][:, c0:c0 + 128], mstrict[:, :], ALU.mult,
                       [('wsb', i % 3)], [('wsb', i % 3)])
                if t['kb'] > 0:
                    TT('dve', t['L'][:, c0:512], t['L'][:, c0:512], spb3[i % 3][:, c0:512], ALU.add,
                       [t['lk'], ('spb', i % 3)], [t['lk']])

            def stC(i):
                t = tinfo(i)
                c0 = t['c0']
                h, G = t['h'], t['G']
                ob, obk = t['ob'], t['obk']
                pr = h // 2
                MM(ob[:, c0:512], vtok[:, t['kb'], 128 * pr:128 * pr + 128], wsb3[i % 3][:, c0:512], t['first'], t['last'],
                   [('vts', t['kb']), ('wsb', i % 3)], [obk], sgc=True)
                if t['last']:
                    gsl = slice(G * 512, (G + 1) * 512)
                    hp = 64 * (h % 2)
                    rows = slice(hp, hp + 64)

                    def hn1():
                        CP('dve', of[rows, :], ob[rows, :], [obk], [('of', hp)])
                        TT('dve', osq[rows, :], of[rows, :], of[rows, :], ALU.mult, [('of', hp)], ['osq'])

                    def hn2():
                        MM(ps[7][:, :], bdones[:, :], osq[:, :], True, True, ['osq'], [('ps', 7)])

                    def hn3():
                        ACT(rst[rows, :], ps[7][rows, :], AF.Ln, [('ps', 7)], [('rst', hp)], scale=1.0 / 64, bias=EPS)
                        ACT(rst[rows, :], rst[rows, :], AF.Exp, [('rst', hp)], [('rst', hp)], scale=-0.5)

                    def hn4():
                        STT('dve', osT[rows, pr, gsl], of[rows, :], gsb[rows, pr:pr + 1], rst[rows, :], ALU.mult, ALU.mult,
                            [('of', hp), ('rst', hp)], [('osT', h, G)])
                    deferred.append([1, hn1])
                    deferred.append([2, hn2])
                    deferred.append([3, hn3])
                    deferred.append([4, hn4])

            deferred = []

            def run_deferred(all_=False):
                for d_ in list(deferred):
                    d_[0] -= 1
                for d_ in list(deferred):
                    if d_[0] <= 0 or all_:
                        d_[1]()
                        deferred.remove(d_)

            for c_ in sb_proj_chunks(0):
                c_()
            for k in range(NT + 2):
                if 4 + k < NTT:
                    v_chunk(4 + k)
                if k < NT:
                    stA(k)
                if 1 <= k <= NT:
                    stB(k - 1)
                run_deferred()
                if k >= 2:
                    stC(k - 2)
            while deferred:
                run_deferred()
            S.flush()
            if 'osT' in dbg_out:
                DMA('pool', dbg_out['osT'], osT[:, :, :], [], ())
                S.flush()
        sB.close()
        if stop_after == 'SB':
            sA.close()
            break

        sR = ExitStack()
        x1 = sb(sR, "x1", [128, NTT, D], F32, side='right')
        h2T = sb(sR, "h2T", [128, 8, 1024], BF, side='right')
        xn = [sb(sR, f"xn{i}", [128, D], BF, side='right') for i in range(2)]
        junk = sb(sR, "junk", [128, D], BF, side='right')

        def n2_front(hf_, i):
            c = hf_ * 8 + i
            tt_ = hf_ * 8 + i
            b_ = i % 2
            src = x1[:, tt_, :]
            skeys = [('x1', tt_)]
            ACT(junk[:, :], src, AF.Square, skeys, ['junk', ('ss', c)], accum_out=ss[:, c:c + 1])
            ACT(rs[:, c:c + 1], ss[:, c:c + 1], AF.Ln, [('ss', c)], [('rs', c)], scale=1.0 / D, bias=EPS)
            ACT(rs[:, c:c + 1], rs[:, c:c + 1], AF.Exp, [('rs', c)], [('rs', c)], scale=-0.5)
            TS('dve', xn[b_][:, :], src, rs[:, c:c + 1], None, ALU.mult, None, skeys + [('rs', c)], [('xn', b_)])

        def n2_back(hf_, i):
            b_ = i % 2
            for kt in range(8):
                TR(psb[b_][:, kt * 128:(kt + 1) * 128], xn[b_][:, kt * 128:(kt + 1) * 128], ident[:, :],
                   [('xn', b_)], [('ps', b_)])
            TT('dve', h2T[:, :, i * 128:(i + 1) * 128],
               psb[b_].rearrange("p (k t) -> p k t", k=8),
               gb2[:, :].rearrange("p (k t) -> p k t", k=8), ALU.mult,
               [('ps', b_)], [('h2T', i)])


        with ExitStack() as st:
            wos = sb(st, "wos", [128, 4, D], BF)
            wogl = sb(st, "wogl", [128, 4, D], BF)
            DMA('pool', wos[:, :, :], w_out_sb_v, (), ['wos'])
            DMA('pool', wogl[:, :, :], w_out_gl_v, (), ['wogl'])
            xt = [sb(st, f"xt{i}", [128, D], F32) for i in range(2)]
            cnt = 0
            MS('dve', ss[:, 0:16], 0.0, SS16)
            for tt in range(NTT):
                b = tt % 2
                tsl = slice(tt * 128, (tt + 1) * 128)
                DMA('sp', xt[b][:, :], x_d[seq, tsl, :], (), [('xt', b)])
                for cg in range(2):
                    bank = 2 + cnt % 4
                    cnt += 1
                    csl = slice(cg * 512, (cg + 1) * 512)
                    for h in range(4):
                        MM(ps[bank][:, :], osT[:, h, tsl], wos[:, h, csl], h == 0, False, ['wos'], [('ps', bank)])
                    for h in range(4):
                        MM(ps[bank][:, :], ogT[:, h, tsl], wogl[:, h, csl], False, h == 3, ['wogl'], [('ps', bank)])
                    TT('dve', x1[:, tt, csl], ps[bank][:, :], xt[b][:, csl], ALU.add, [('ps', bank), ('xt', b)], [('x1', tt)])
                if tt < 8:
                    n2_front(0, tt)
                if 1 <= tt <= 8:
                    n2_back(0, tt - 1)
            if 'x1' in dbg_out:
                dump('x1', x1[:, :, :], [('x1', i) for i in range(NTT)])
            S.flush()
        sA.close()
        if stop_after == 'P3':
            sR.close()
            break

        with ExitStack() as st:
            gT = sb(st, "gT", [128, 22, 1024], BF)
            wd = sb(st, "wd", [128, 22, 512], BF)
            ARN = 3 * 2048 + 3 * 1028 + 4 * 1024 + 1024
            arena = sb(st, "arena", [128, ARN], BF)
            off = 0
            wup = []
            for i_ in range(3):
                wup.append(arena[:, off:off + 2048].rearrange("p (k c) -> p k c", k=8))
                off += 2048
            ybuf = []
            for i_ in range(3):
                ybuf.append(arena[:, off:off + 1028].bitcast(F32))
                off += 1028
            acc = []
            for i_ in range(4):
                acc.append(arena[:, off:off + 1024].bitcast(F32))
                off += 1024
            sa = arena[:, off:off + 1024].bitcast(F32)
            wd2 = arena[:, 0:22 * 512].rearrange("p (j c) -> p j c", j=22)
            ALIAS = ([('wup', i_, p_) for i_ in range(3) for p_ in range(2)] + [('yb', i_) for i_ in range(3)]
                     + [('acc', i_) for i_ in range(4)] + ['sa'])
            xo = [sb(st, f"xo{i}", [128, D], F32) for i in range(2)]
            ssF = sb(st, "ssF", [128, 16], F32)
            rsF = sb(st, "rsF", [128, 16], F32)
            if seq == 0:
                print("P5 sbuf bytes remaining", nc.sbuf_bytes_remaining, flush=True)

            MS('dve', ssF[:, 0:16], 0.0, [('ssF', i_) for i_ in range(16)])
            cnt = 0
            yi = 0
            ai = 0
            for hf in range(2):
                def ld_wup(jj):
                    wb_ = jj % 3
                    DMA('pool', wup[wb_][:, :, 0:128], w_up_v[:, :, 128 * jj:128 * jj + 128], (), [('wup', wb_, 0)])
                    DMA('pool', wup[wb_][:, :, 128:256], w_up_v[:, :, DFF + 128 * jj:DFF + 128 * jj + 128], (), [('wup', wb_, 1)])
                ld_wup(0)
                ld_wup(1)
                for j in range(22):
                    wb = j % 3
                    if j + 2 < 22:
                        ld_wup(j + 2)
                    if j == 2:
                        DMA('pool', wd[:, :, :], w_dn_v[:, :, 0:512], (), ['wd'])
                    for g2 in range(2):
                        first_group = (hf == 0 and g2 == 0)
                        last_group = (hf == 1 and g2 == 1)
                        accs = []
                        for part in range(2):
                            ct = part * 22 + j
                            bank = 2 + cnt % 4
                            cnt += 1
                            for kt in range(8):
                                MM(ps[bank][:, :], wup[wb][:, kt, part * 128:(part + 1) * 128],
                                   h2T[:, kt, g2 * 512:(g2 + 1) * 512], kt == 0, kt == 7,
                                   [('wup', wb, part)] + hkeys('h2T', g2 * 512, g2 * 512 + 512), [('ps', bank)])
                            yb = ybuf[yi % 3]
                            ykey = ('yb', yi % 3)
                            yi += 1
                            ACT(yb[:, 2:514], ps[bank][:, :], AF.Copy, [('ps', bank)], [ykey])
                            if first_group:
                                MS('pool', yb[:, 0:2], 0.0, [ykey])
                            else:
                                CP('pool', yb[:, 0:2], carry[:, ct, :], [('carry', ct)], [ykey])
                            if not last_group:
                                CP('pool', carry[:, ct, :], yb[:, 512:514], [ykey], [('carry', ct)])
                            a = acc[ai % 4]
                            akey = ('acc', ai % 4)
                            ai += 1
                            ACT(a[:, :], ps[bank][:, :], AF.Identity, [('ps', bank)], [akey],
                                scale=cw[:, 88 + ct:88 + ct + 1], bias=cb[:, ct:ct + 1])
                            STT('dve', a[:, :], yb[:, 1:513], cw[:, 44 + ct:44 + ct + 1], a[:, :], ALU.mult, ALU.add,
                                [ykey, akey], [akey])
                            STT('dve', a[:, :], yb[:, 0:512], cw[:, ct:ct + 1], a[:, :], ALU.mult, ALU.add,
                                [ykey, akey], [akey])
                            accs.append((a, akey))
                        ACT(sa[:, :], accs[0][0][:, :], AF.Silu, [accs[0][1]], ['sa'])
                        TT('dve', gT[:, j, g2 * 512:(g2 + 1) * 512], sa[:, :], accs[1][0][:, :], ALU.mult,
                           ['sa', accs[1][1]], [('gT', j, g2)])
                DMA('pool', wd2[:, :, :], w_dn_v[:, :, 512:1024], (), ['wd2'] + ALIAS)
                for cg in range(2):
                    csl = slice(cg * 512, (cg + 1) * 512)
                    wdt = wd if cg == 0 else wd2
                    wkeys = ['wd'] if cg == 0 else (['wd2'] + ALIAS)
                    for i in range(8):
                        tt = hf * 8 + i
                        bank = 6 + i % 2
                        if hf == 0 and cg == 0:
                            n2_front(1, i)
                        for j in range(22):
                            MM(ps[bank][:, :], gT[:, j, i * 128:(i + 1) * 128], wdt[:, j, :], j == 0, j == 21,
                               [('gT', j, i // 4)] + wkeys, [('ps', bank)])
                        TT('dve', x1[:, tt, csl], ps[bank][:, :], x1[:, tt, csl], ALU.add, [('ps', bank), ('x1', tt)], [('x1', tt)])
                        if hf == 0 and cg == 0:
                            n2_back(1, i)
                        if cg == 1:
                            c = hf * 8 + i
                            b = i % 2
                            tsl = slice(tt * 128, (tt + 1) * 128)
                            ACT(junk[:, :], x1[:, tt, :], AF.Square, [('x1', tt)], ['junk', ('ssF', c)], accum_out=ssF[:, c:c + 1])
                            ACT(rsF[:, c:c + 1], ssF[:, c:c + 1], AF.Ln, [('ssF', c)], [('rsF', c)], scale=1.0 / D, bias=EPS)
                            ACT(rsF[:, c:c + 1], rsF[:, c:c + 1], AF.Exp, [('rsF', c)], [('rsF', c)], scale=-0.5)
                            STT('dve', xo[b][:, :], x1[:, tt, :], rsF[:, c:c + 1], fing[:, :], ALU.mult, ALU.mult,
                                [('x1', tt), ('rsF', c)], [('xo', b)])
                            DMA('sp', out_d[seq, tsl, :], xo[b][:, :], [('xo', b)], ())
            S.flush()
        sR.close()

    top.close()
    return nc


_NC_CACHE = {}


def _prep_inputs(inputs):
    f = lambda a: np.ascontiguousarray(np.asarray(a, dtype=np.float32))
    g1 = f(inputs["attn_norm_g"])[0]
    g2 = f(inputs["ffn_norm_g"])[0]
    shared = {
        "w_in": f(inputs["w_in"])[0],
        "w_out": f(inputs["w_out"])[0],
        "w_up": f(inputs["w_ffn_up"])[0],
        "w_dn": f(inputs["w_ffn_down"])[0],
        "wgu": f(inputs["w_gate_up"])[0],
        "bgu": f(inputs["b_gate_up"])[0].reshape(1, 256),
        "gb1": np.ascontiguousarray(np.repeat(g1.reshape(8, 128).T[:, :, None], 128, axis=2).reshape(128, D)),
        "gb2": np.ascontiguousarray(np.repeat(g2.reshape(8, 128).T[:, :, None], 128, axis=2).reshape(128, D)),
        "fing": np.ascontiguousarray(np.broadcast_to(f(inputs["final_norm_g"])[None, :], (128, D))),
        "gsb": np.ascontiguousarray(f(inputs["sb_out_g"])[0].reshape(4, 128).T),
        "ggl": np.ascontiguousarray(f(inputs["gla_out_g"])[0].reshape(4, 128).T),
        "cw": np.ascontiguousarray(f(inputs["conv_w"])[0].reshape(3, 44, 128).transpose(2, 0, 1).reshape(128, 132)),
        "cb": np.ascontiguousarray(f(inputs["conv_b"])[0].reshape(44, 128).T),
    }
    for k, v in _consts().items():
        shared["c_" + k] = v
    return shared


def kernel(**inputs):
    x = np.ascontiguousarray(np.asarray(inputs["x"], dtype=np.float32))
    B = x.shape[0]
    nseq = B // NCORES
    shared = _prep_inputs(inputs)
    if nseq not in _NC_CACHE:
        _NC_CACHE[nseq] = build(nseq)
    nc = _NC_CACHE[nseq]
    in_maps = []
    for c in range(NCORES):
        m = dict(shared)
        m["x"] = x[c * nseq:(c + 1) * nseq]
        in_maps.append(m)
    res = run_bass_kernel_spmd(nc, in_maps, core_ids=list(range(NCORES)))
    return np.concatenate([r["out"] for r in res.results], axis=0)
```

```python
import numpy as np
from contextlib import ExitStack
import concourse.bass as bass
import concourse.mybir as mybir
from concourse.bass_utils import run_bass_kernel_spmd

F32 = mybir.dt.float32
BF = mybir.dt.bfloat16
AF = mybir.ActivationFunctionType
ALU = mybir.AluOpType

D = 1024
SEQ = 2048
NCORES = 8
DFF = 2816
EPS = 1e-6
NTT = SEQ // 128

ENGS = ('pe', 'act', 'dve', 'pool', 'sp')
EIDX = {e: i for i, e in enumerate(ENGS)}


class Sched:
    R = 4
    KD = 12

    def __init__(self, nc, es):
        self.nc = nc
        self.esem = {e: [es.enter_context(nc.semaphore(f"s_{e}{i}")) for i in range(self.R)]
                     for e in ENGS[:4]}
        self.dsem = {e: [es.enter_context(nc.semaphore(f"d_{e}{i}")) for i in range(self.KD)]
                     for e in ('sp', 'pool')}
        self.cnt = {e: 0 for e in ENGS}
        self.clock = {e: [0] * 5 for e in ENGS}
        self.snap = {e: [None] for e in ENGS}
        self.dcnt = {e: 0 for e in self.dsem}
        self.dseen = {e: {} for e in ENGS}
        self.lastw = {}
        self.readers = {}
        self.prog = {e: [] for e in ENGS}
        self.nwaits = 0
        self.nops = 0

    def _deps(self, eng, reads, writes):
        deps = set()
        for r in reads:
            ev = self.lastw.get(r)
            if ev is not None:
                deps.add(ev)
        for w in writes:
            ev = self.lastw.get(w)
            if ev is not None:
                deps.add(ev)
            for ev in self.readers.get(w, ()):
                deps.add(ev)
        best = {}
        for d in deps:
            if d[0] == 'c':
                _, e2, k = d
                if e2 == eng and eng == 'pe':
                    continue
                if self.clock[eng][EIDX[e2]] >= k:
                    continue
                if best.get(e2, 0) < k:
                    best[e2] = k
            else:
                _, pe_, idx, val = d
                if self.dseen[eng].get((pe_, idx), 0) >= val:
                    continue
                self._wait_dma(eng, pe_, idx, val)
        for e2, k in best.items():
            if self.clock[eng][EIDX[e2]] >= k:
                continue
            self._wait_eng(eng, e2, k)

    def _semval(self, k):
        r = k % self.R
        first = r if r != 0 else self.R
        return (k - first) // self.R + 1

    def _wait_eng(self, eng, e2, k):
        sem = self.esem[e2][k % self.R]
        self.prog[eng].append(('w', sem, self._semval(k)))
        self.nwaits += 1
        sn = self.snap[e2][k]
        c = self.clock[eng]
        for i in range(5):
            if sn[i] > c[i]:
                c[i] = sn[i]

    def _wait_dma(self, eng, pe_, idx, val):
        self.prog[eng].append(('w', self.dsem[pe_][idx], val))
        self.nwaits += 1
        self.dseen[eng][(pe_, idx)] = val

    def _record(self, ev, reads, writes):
        for w in writes:
            self.lastw[w] = ev
            self.readers[w] = []
        for r in reads:
            lst = self.readers.setdefault(r, [])
            if ev[0] == 'c':
                lst[:] = [x for x in lst if not (x[0] == 'c' and x[1] == ev[1])]
            lst.append(ev)

    def op(self, eng, fn, reads=(), writes=()):
        self._deps(eng, reads, writes)
        self.cnt[eng] += 1
        k = self.cnt[eng]
        sn = list(self.clock[eng])
        sn[EIDX[eng]] = k
        self.snap[eng].append(tuple(sn))
        self.prog[eng].append(('o', fn, self.esem[eng][k % self.R], 1))
        self.nops += 1
        self._record(('c', eng, k), reads, writes)

    def dma(self, eng, fn, reads=(), writes=()):
        self._deps(eng, reads, writes)
        d = self.dcnt[eng]
        self.dcnt[eng] += 1
        idx = d % self.KD
        prev = 16 * (d // self.KD)
        if prev > 0 and self.dseen[eng].get((eng, idx), 0) < prev:
            self._wait_dma(eng, eng, idx, prev)
        self.prog[eng].append(('o', fn, self.dsem[eng][idx], 16))
        self.nops += 1
        self._record(('d', eng, idx, prev + 16), reads, writes)

    def _final_vals(self):
        out = {}
        for pe_ in self.dsem:
            n = self.dcnt[pe_]
            for idx in range(min(n, self.KD)):
                out[(pe_, idx)] = 16 * ((n - idx + self.KD - 1) // self.KD)
        return out

    def flush(self):
        nc = self.nc
        fv = self._final_vals()
        for (pe_, idx), val in fv.items():
            if self.dseen['sp'].get((pe_, idx), 0) < val:
                self._wait_dma('sp', pe_, idx, val)
        prog = self.prog

        def mk(name):
            items = prog[name]

            def f(e):
                for it in items:
                    if it[0] == 'w':
                        e.wait_ge(it[1], it[2])
                    else:
                        it[1](e).then_inc(it[2], it[3])
            return f

        with nc.Block(no_gpsimd_drain=True) as block:
            block.tensor(mk('pe'))
            block.scalar(mk('act'))
            block.vector(mk('dve'))
            block.gpsimd(mk('pool'))
            block.sync(mk('sp'))
        self.prog = {e: [] for e in ENGS}
        full = [self.cnt[e] for e in ENGS]
        for e in ENGS:
            self.clock[e] = list(full)
            self.dseen[e] = dict(fv)
        self.lastw = {}
        self.readers = {}


def _consts():
    i = np.arange(128)
    a, b = i[:, None], i[None, :]
    same = (a // 64) == (b // 64)
    c = {}
    c["ident"] = np.eye(128, dtype=np.float32)
    c["triinc_neg"] = -(a >= b).astype(np.float32)
    c["mask_strict"] = (a < b).astype(np.float32)
    gc = -1.0 / 16.0
    c["gl1"] = (gc * ((a > b) & same)).astype(np.float32)
    r2 = np.zeros((128, 130), np.float32)
    r2[:, :128] = gc * ((a <= b) & same)
    r2[:, 128] = gc * (i < 64)
    r2[:, 129] = gc * (i >= 64)
    c["gr2"] = r2
    c["mask_gla"] = ((a <= b) & same).astype(np.float32)
    c["bdones"] = same.astype(np.float32)
    return c


CONST_SHAPES = {"ident": (128, 128), "triinc_neg": (128, 128), "mask_strict": (128, 128),
                "gl1": (128, 128), "gr2": (128, 130), "mask_gla": (128, 128), "bdones": (128, 128)}


class StopBuild(Exception):
    pass


def build(nseq, stop_after=None, dbg=()):
    try:
        return _build(nseq, stop_after, dbg)
    except StopBuild as e:
        return e.args[0]


def _build(nseq, stop_after=None, dbg=()):
    nc = bass.Bass("TRN2", target_bir_lowering=False)
    import os as _os
    _stopk = int(_os.environ.get("STOPK", "0"))
    _ckn = [0]
    dbg = dict(dbg)
    dbg_out = {}

    def din(name, shape, dt=F32):
        return nc.dram_tensor(name, list(shape), dt, kind="ExternalInput").ap()

    x_d = din("x", [nseq, SEQ, D])
    w_in_d = din("w_in", [D, 3088])
    w_out_d = din("w_out", [D, D])
    w_up_d = din("w_up", [D, 2 * DFF])
    w_dn_d = din("w_dn", [DFF, D])
    wgu_d = din("wgu", [16, 256])
    bgu_d = din("bgu", [1, 256])
    gb1_d = din("gb1", [128, D])
    gb2_d = din("gb2", [128, D])
    fing_d = din("fing", [128, D])
    gsb_d = din("gsb", [128, 4])
    ggl_d = din("ggl", [128, 4])
    cw_d = din("cw", [128, 3 * 44])
    cb_d = din("cb", [128, 44])
    cst_d = {k: din("c_" + k, v) for k, v in CONST_SHAPES.items()}
    out_d = nc.dram_tensor("out", [nseq, SEQ, D], F32, kind="ExternalOutput").ap()
    for name, shape in dbg.items():
        dbg_out[name] = nc.dram_tensor("dbg_" + name, list(shape), F32, kind="ExternalOutput").ap()

    w_in_v = w_in_d.rearrange("(kt p) c -> p kt c", p=128)
    w_up_v = w_up_d.rearrange("(kt p) c -> p kt c", p=128)
    w_dn_v = w_dn_d.rearrange("(j p) c -> p j c", p=128)
    w_out_sb_v = w_out_d[0:512, :].rearrange("(h p) c -> p h c", p=128)
    w_out_gl_v = w_out_d[512:1024, :].rearrange("(h p) c -> p h c", p=128)

    top = ExitStack()
    S = Sched(nc, top)
    manual = []

    def ck(tag=""):
        _ckn[0] += 1
        if _stopk and _ckn[0] == _stopk:
            print("STOP at checkpoint", _ckn[0], tag, flush=True)
            S.flush()
            raise StopBuild(nc)

    uid = [0]

    def sb(stack, name, shape, dt, side=None):
        uid[0] += 1
        name = f"t{uid[0]}_{name}"
        if side is None:
            return stack.enter_context(nc.sbuf_tensor(name, list(shape), dt))
        return stack.enter_context(nc.sbuf_tensor(name, list(shape), dt, side=side))

    def ACT(out, in_, func, r, w, **kw):
        S.op('act', lambda e: e.activation(out=out, in_=in_, func=func, **kw), r, w)

    def TS(eng, out, in0, s1, s2, op0, op1, r, w):
        if s2 is None:
            S.op(eng, lambda e: e.tensor_scalar(out=out, in0=in0, scalar1=s1, scalar2=None, op0=op0), r, w)
        else:
            S.op(eng, lambda e: e.tensor_scalar(out=out, in0=in0, scalar1=s1, scalar2=s2, op0=op0, op1=op1), r, w)

    def TT(eng, out, in0, in1, op, r, w):
        S.op(eng, lambda e: e.tensor_tensor(out=out, in0=in0, in1=in1, op=op), r, w)

    def STT(eng, out, in0, scalar, in1, op0, op1, r, w):
        S.op(eng, lambda e: e.scalar_tensor_tensor(out=out, in0=in0, scalar=scalar, in1=in1, op0=op0, op1=op1), r, w)

    def CP(eng, out, in_, r, w):
        S.op(eng, lambda e: e.tensor_copy(out=out, in_=in_), r, w)

    def MS(eng, ap, val, w):
        S.op(eng, lambda e: e.memset(ap, val), (), w)

    def MM(out, lhsT, rhs, start, stop, r, w, sgc=False):
        if sgc:
            S.op('pe', lambda e: e.matmul(out, lhsT, rhs, start=start, stop=stop, skip_group_check=True), r, w)
        else:
            S.op('pe', lambda e: e.matmul(out, lhsT, rhs, start=start, stop=stop), r, w)

    def TR(out, in_, ident, r, w):
        S.op('pe', lambda e: e.transpose(out, in_, ident), r, w)

    def DMA(eng, out, in_, r, w):
        S.dma(eng, lambda e: e.dma_start(out=out, in_=in_), r, w)

    def dump(name, ap, keys, rows=None):
        if name in dbg_out:
            DMA('sp', dbg_out[name] if rows is None else dbg_out[name][rows], ap, keys, ())

    ps = [top.enter_context(nc.psum_tensor(f"ps{i}", [128, 512], F32)) for i in range(8)]
    psb = [ps[i][:, :].bitcast(BF) for i in range(2)]

    ident = sb(top, "ident", [128, 128], BF)
    triinc = sb(top, "triinc", [128, 128], BF)
    onesneg = sb(top, "onesneg", [128, 128], BF)
    mstrict = sb(top, "mstrict", [128, 128], BF)
    gl1 = sb(top, "gl1", [128, 128], F32)
    gr2 = sb(top, "gr2", [128, 130], F32)
    mgla = sb(top, "mgla", [128, 128], F32)
    onesf = sb(top, "onesf", [128, 128], F32)
    onesrow = sb(top, "onesrow", [1, 128], BF)
    bgu = sb(top, "bgu", [1, 256], BF)
    wgu = sb(top, "wgu", [16, 256], BF)
    gb1 = sb(top, "gb1", [128, D], F32)
    gb2 = sb(top, "gb2", [128, D], F32)
    fing = sb(top, "fing", [128, D], F32)
    gsb = sb(top, "gsb", [128, 4], F32)
    bdones = sb(top, "bdones", [128, 128], BF)
    onesb = sb(top, "onesb", [128, 128], BF)
    ggl = sb(top, "ggl", [128, 4], F32)
    cw = sb(top, "cw", [128, 3 * 44], F32)
    cb = sb(top, "cb", [128, 44], F32)
    ss = sb(top, "ss", [128, 16], F32)
    rs = sb(top, "rs", [128, 16], F32)

    DMA('pool', ident[:, :], cst_d["ident"], (), ['ident'])
    DMA('pool', triinc[:, :], cst_d["triinc_neg"], (), ['triinc'])
    DMA('pool', mstrict[:, :], cst_d["mask_strict"], (), ['mstrict'])
    DMA('pool', bgu[:, :], bgu_d, (), ['bgu'])
    DMA('pool', wgu[:, :], wgu_d, (), ['wgu'])
    DMA('sp', gl1[:, :], cst_d["gl1"], (), ['gl1'])
    DMA('sp', gr2[:, :], cst_d["gr2"], (), ['gr2'])
    DMA('sp', mgla[:, :], cst_d["mask_gla"], (), ['mgla'])
    DMA('sp', gb1[:, :], gb1_d, (), ['gb1'])
    DMA('sp', gb2[:, :], gb2_d, (), ['gb2'])
    DMA('sp', fing[:, :], fing_d, (), ['fing'])
    DMA('sp', gsb[:, :], gsb_d, (), ['gsb'])
    DMA('pool', bdones[:, :], cst_d["bdones"], (), ['bdones'])
    MS('dve', onesb[:, :], 1.0, ['onesb'])
    DMA('sp', ggl[:, :], ggl_d, (), ['ggl'])
    DMA('sp', cw[:, :], cw_d, (), ['cw'])
    DMA('sp', cb[:, :], cb_d, (), ['cb'])
    MS('dve', onesneg[:, :], -1.0, ['onesneg'])
    MS('dve', onesf[:, :], 1.0, ['onesf'])
    MS('dve', onesrow[:, :], 1.0, ['onesrow'])
    S.flush()

    def norm_T(st, srcs, gbt, dstT, dkey):
        xn = [sb(st, f"xn{i}", [128, D], BF) for i in range(2)]
        junk = sb(st, "junk", [128, D], BF)
        n = len(srcs)
        MS('dve', ss[:, 0:n], 0.0, ['ss'])
        for i, (src, skeys) in enumerate(srcs):
            b = i % 2
            ACT(junk[:, :], src, AF.Square, skeys, ['junk', 'ss'], accum_out=ss[:, i:i + 1])
            ACT(rs[:, i:i + 1], ss[:, i:i + 1], AF.Ln, ['ss'], [('rs', i)], scale=1.0 / D, bias=EPS)
            ACT(rs[:, i:i + 1], rs[:, i:i + 1], AF.Exp, [('rs', i)], [('rs', i)], scale=-0.5)
            TS('dve', xn[b][:, :], src, rs[:, i:i + 1], None, ALU.mult, None, list(skeys) + [('rs', i)], [('xn', b)])
            for kt in range(8):
                TR(psb[b][:, kt * 128:(kt + 1) * 128], xn[b][:, kt * 128:(kt + 1) * 128], ident[:, :],
                   [('xn', b), 'ident'], [('ps', b)])
            TT('dve', dstT[:, :, i * 128:(i + 1) * 128],
               psb[b].rearrange("p (k t) -> p k t", k=8),
               gbt[:, :].rearrange("p (k t) -> p k t", k=8), ALU.mult,
               [('ps', b), 'gbt'], [(dkey, i)])

    def hkeys(key, c0, c1):
        return [(key, i) for i in range(c0 // 128, (c1 + 127) // 128)]

    pj = [0]

    def proj_fm(W, wkey, wc0, M, srcT, skey, ntok, evac):
        for tg in range(ntok // 512):
            bank = 2 + (pj[0] % 2)
            pj[0] += 1
            for kt in range(8):
                MM(ps[bank][0:M, :], W[:, kt, wc0:wc0 + M], srcT[:, kt, tg * 512:(tg + 1) * 512],
                   kt == 0, kt == 7, [wkey] + hkeys(skey, tg * 512, tg * 512 + 512), [('ps', bank)])
            evac(tg, ps[bank][0:M, :], [('ps', bank)])

    carry = sb(top, "carry", [128, 44, 2], F32)
    SS16 = [('ss', i) for i in range(16)]

    for seq in range(nseq):
        sA = ExitStack()
        osT = sb(sA, "osT", [128, 4, SEQ], BF)
        ogT = sb(sA, "ogT", [128, 4, SEQ], BF)
        sB = ExitStack()
        hT = sb(sB, "hT", [128, 8, SEQ], BF)

        with ExitStack() as st:
            wk = sb(st, "wk", [128, 8, 256], BF)
            wq = sb(st, "wq", [128, 8, 256], BF)
            wog = sb(st, "wog", [128, 8, 512], BF)
            vtok = sb(st, "vtok", [128, 16, 512], BF)
            kend = sb(st, "kend", [128, 16, 256], BF)
            sptok = sb(st, "sptok", [128, 16, 256], F32)
            with ExitStack() as stA_:
                wv = sb(stA_, "wv", [128, 8, 512], BF)
                wlr = sb(stA_, "wlr", [128, 8, 16], BF)
                glrT = sb(stA_, "glrT", [16, SEQ], BF)
                e1 = [sb(stA_, f"e1{i}", [128, 256], F32) for i in range(2)]
                erb = [sb(stA_, f"erb{i}", [128, 256], F32) for i in range(2)]
                DMA('pool', wlr[:, :, :], w_in_v[:, :, 2560:2576], (), ['wlr'])
                DMA('pool', wk[:, :, :], w_in_v[:, :, 1792:2048], (), ['wk'])
                DMA('pool', wv[:, :, :], w_in_v[:, :, 2048:2560], (), ['wv'])
                DMA('pool', wq[:, :, :], w_in_v[:, :, 1536:1792], (), ['wq'])
                DMA('pool', wog[:, :, :], w_in_v[:, :, 2576:3088], (), ['wog'])
                xt = [sb(stA_, f"xt{i}", [128, D], F32) for i in range(4)]
                xn = [sb(stA_, f"xn{i}", [128, D], BF) for i in range(2)]
                junk = sb(stA_, "junk", [128, D], BF)
                MS('dve', ss[:, 0:16], 0.0, SS16)

                def p1_tile(i):
                    b = i % 2
                    xb = i % 4
                    DMA('sp', xt[xb][:, :], x_d[seq, i * 128:(i + 1) * 128, :], (), [('xt', xb)])
                    src = xt[xb][:, :]
                    skeys = [('xt', xb)]
                    ACT(junk[:, :], src, AF.Square, skeys, ['junk', ('ss', i)], accum_out=ss[:, i:i + 1])
                    ACT(rs[:, i:i + 1], ss[:, i:i + 1], AF.Ln, [('ss', i)], [('rs', i)], scale=1.0 / D, bias=EPS)
                    ACT(rs[:, i:i + 1], rs[:, i:i + 1], AF.Exp, [('rs', i)], [('rs', i)], scale=-0.5)
                    TS('dve', xn[b][:, :], src, rs[:, i:i + 1], None, ALU.mult, None, skeys + [('rs', i)], [('xn', b)])
                    for kt in range(8):
                        TR(psb[b][:, kt * 128:(kt + 1) * 128], xn[b][:, kt * 128:(kt + 1) * 128], ident[:, :],
                           [('xn', b)], [('ps', b)])
                    TT('dve', hT[:, :, i * 128:(i + 1) * 128],
                       psb[b].rearrange("p (k t) -> p k t", k=8),
                       gb1[:, :].rearrange("p (k t) -> p k t", k=8), ALU.mult,
                       [('ps', b)], [('hT', i)])

                def glr_group(tg):
                    bank = 2 + (pj[0] % 2)
                    pj[0] += 1
                    for kt in range(8):
                        MM(ps[bank][0:16, :], wlr[:, kt, 0:16], hT[:, kt, tg * 512:(tg + 1) * 512],
                           kt == 0, kt == 7, ['wlr'] + hkeys('hT', tg * 512, tg * 512 + 512), [('ps', bank)])
                    ACT(glrT[0:16, tg * 512:(tg + 1) * 512], ps[bank][0:16, :], AF.Copy, [('ps', bank)], [('glrT', tg)])

                def glaA_tile(tt):
                    b = tt % 2
                    tsl = slice(tt * 128, (tt + 1) * 128)
                    bp, brb = (4, 5) if b == 0 else (2, 3)
                    bk_, bv_ = (6, 7)
                    MM(ps[bp][:, 0:256], glrT[0:16, tsl], wgu[0:16, :], True, False, [('glrT', tt // 4)], [('ps', bp)])
                    MM(ps[bp][:, 0:256], onesrow[0:1, :], bgu[0:1, :], False, True, [], [('ps', bp)])
                    ACT(e1[b][:, :], ps[bp][:, 0:256], AF.Exp, [('ps', bp)], [('e1', b)], scale=-1.0)
                    ACT(sptok[:, tt, :], e1[b][:, :], AF.Ln, [('e1', b)], [('sp', tt)], bias=1.0)
                    MM(ps[brb][:, 0:256], gl1[:, :], sptok[:, tt, :], True, True, [('sp', tt)], [('ps', brb)])
                    ACT(erb[b][:, :], ps[brb][:, 0:256], AF.Exp, [('ps', brb)], [('erb', b)])
                    for kt in range(8):
                        MM(ps[bk_][:, 0:256], hT[:, kt, tsl], wk[:, kt, :], kt == 0, kt == 7, [('hT', tt), 'wk'], [('ps', bk_)])
                    TT('dve', kend[:, tt, :], ps[bk_][:, 0:256], erb[b][:, :], ALU.mult, [('ps', bk_), ('erb', b)], [('kend', tt)])
                    for kt in range(8):
                        MM(ps[bv_][:, :], hT[:, kt, tsl], wv[:, kt, :], kt == 0, kt == 7, [('hT', tt), 'wv'], [('ps', bv_)])
                    CP('dve', vtok[:, tt, :], ps[bv_][:, :], [('ps', bv_)], [('vtok', tt)])

                for i in range(NTT + 5):
                    if i < NTT:
                        p1_tile(i)
                    if 1 <= i <= NTT and i % 4 == 0:
                        glr_group(i // 4 - 1)
                    if i >= 5:
                        glaA_tile(i - 5)
                S.flush()

            gq = [sb(st, f"gq{i}", [64, SEQ], BF) for i in range(2)]
            gk = [sb(st, f"gk{i}", [64, SEQ], BF) for i in range(2)]
            sog = [sb(st, f"sog{i}", [128, SEQ], BF) for i in range(2)]
            Sall = [sb(st, f"Sall{i}", [64, 32, 128], BF) for i in range(2)]
            Sf = [[sb(st, f"Sf{i}{j}", [64, 128], F32) for j in range(2)] for i in range(2)]
            dec = [sb(st, f"dec{i}", [64, 32], F32) for i in range(2)]
            eb = [[sb(st, f"eb{i}{j}", [64, 128], F32) for j in range(2)] for i in range(2)]
            enb = [[sb(st, f"enb{i}{j}", [64, 128], F32) for j in range(2)] for i in range(2)]
            attn = [[sb(st, f"attn{i}{j}", [128, 128], BF) for j in range(2)] for i in range(2)]
            osq = [sb(st, f"osq{i}", [128, 512], BF) for i in range(2)]
            of = [sb(st, f"of{i}", [128, 512], F32) for i in range(2)]
            rst = [sb(st, f"rst{i}", [128, 512], F32) for i in range(2)]
            if seq == 0:
                print("GLA sbuf bytes remaining", nc.sbuf_bytes_remaining, flush=True)

            def proj_group(W, wkey, wc0, M, tg, evac):
                bank = 2 + (pj[0] % 2)
                pj[0] += 1
                for kt in range(8):
                    MM(ps[bank][0:M, :], W[:, kt, wc0:wc0 + M], hT[:, kt, tg * 512:(tg + 1) * 512],
                       kt == 0, kt == 7, [wkey], [('ps', bank)])
                evac(tg, ps[bank][0:M, :], [('ps', bank)])

            for pp in range(2):
                def bT_steps(tg_):
                    for tt in range(4 * tg_, 4 * tg_ + 4):
                        b = tt % 2
                        tg = tt // 4
                        tsl = slice(tt * 128, (tt + 1) * 128)
                        for q in range(2):
                            h = 2 * pp + q
                            bb = (4 + q) if tt % 2 == 0 else (6 + q)
                            MM(ps[bb][0:64, 0:130], sptok[:, tt, 64 * h:64 * h + 64], gr2[:, :], True, True, [('sp', tt)], [('ps', bb)])
                            ACT(eb[q][b][:, :], ps[bb][0:64, 0:128], AF.Exp, [('ps', bb)], [('eb', q, b)])
                            ACT(enb[q][b][:, :], ps[bb][0:64, 0:128], AF.Exp, [('ps', bb)], [('enb', q, b)], scale=-1.0)
                            ACT(dec[q][:, 2 * tt:2 * tt + 2], ps[bb][0:64, 128:130], AF.Exp, [('ps', bb)], [('dec', q, tt)])
                            TT('dve', gq[q][0:64, tsl], gq[q][0:64, tsl], eb[q][b][:, :], ALU.mult, [('gq', q, tg), ('eb', q, b)], [('gq', q, tg)])
                            TT('dve', gk[q][0:64, tsl], gk[q][0:64, tsl], enb[q][b][:, :], ALU.mult, [('gk', q, tg), ('enb', q, b)], [('gk', q, tg)])

                for tg_ in range(4):
                    for q in range(2):
                        h = 2 * pp + q

                        def ev_q(tg, p, pk, q=q):
                            ACT(gq[q][0:64, tg * 512:(tg + 1) * 512], p, AF.Copy, pk, [('gq', q, tg)], scale=0.125)

                        def ev_k(tg, p, pk, q=q):
                            ACT(gk[q][0:64, tg * 512:(tg + 1) * 512], p, AF.Copy, pk, [('gk', q, tg)])
                        proj_group(wq, 'wq', 64 * h, 64, tg_, ev_q)
                        proj_group(wk, 'wk', 64 * h, 64, tg_, ev_k)
                    if tg_ >= 1:
                        bT_steps(tg_ - 1)
                bT_steps(3)
                for q in range(2):
                    h = 2 * pp + q

                    def ev_og(tg, p, pk, q=q):
                        ACT(sog[q][:, tg * 512:(tg + 1) * 512], p, AF.Silu, pk, [('sog', q, tg)])
                    proj_fm(wog, 'wog', 128 * h, 128, hT, 'hT', SEQ, ev_og)
                for q in range(2):
                    MS('dve', Sf[q][0][:, :], 0.0, [('Sf', q, 0)])
                    MS('dve', Sall[q][:, 0, :], 0.0, [('Sall', q, 0)])
                for n in range(32):
                    tt, half = n // 2, n % 2
                    pr = slice(64 * half, 64 * half + 64)
                    for q in range(2):
                        h = 2 * pp + q
                        kbank = ((2, 3), (6, 7))[q][n % 2]
                        kvp = ps[kbank][0:64, 0:128]
                        MM(kvp, kend[pr, tt, 64 * h:64 * h + 64], vtok[pr, tt, 128 * h:128 * h + 128], True, True,
                           [('kend', tt), ('vtok', tt)], [('ps', kbank)])
                        cur, prev = Sf[q][(n + 1) % 2], Sf[q][n % 2]
                        STT('dve', cur[:, :], prev[:, :], dec[q][:, n:n + 1], kvp, ALU.mult, ALU.add,
                            [('Sf', q, n % 2), ('dec', q, tt), ('ps', kbank)], [('Sf', q, (n + 1) % 2)])
                        if n < 31:
                            ACT(Sall[q][:, n + 1, :], cur[:, :], AF.Copy, [('Sf', q, (n + 1) % 2)], [('Sall', q, n + 1)])
                gdef = []

                def g_run():
                    for d_ in list(gdef):
                        d_[0] -= 1
                    for d_ in list(gdef):
                        if d_[0] <= 0:
                            d_[1]()
                            gdef.remove(d_)

                for tg in range(4):
                    gsl = slice(tg * 512, (tg + 1) * 512)
                    for qq in range(4):
                        tt = tg * 4 + qq
                        b = tt % 2
                        tsl = slice(tt * 128, (tt + 1) * 128)
                        for q in range(2):
                            h = 2 * pp + q
                            ab = ps[6 + q]
                            abk = ('ps', 6 + q)
                            obn = q if tg % 2 == 0 else 4 + q
                            ob = ps[obn]
                            obk = ('ps', obn)
                            MM(ab[:, 0:128], gk[q][0:64, tsl], gq[q][0:64, tsl], True, True, [('gk', q, tg), ('gq', q, tg)], [abk])
                            TT('dve', attn[q][b][:, :], ab[:, 0:128], mgla[:, :], ALU.mult, [abk], [('attn', q, b)])
                            MM(ob[:, qq * 128:(qq + 1) * 128], vtok[:, tt, 128 * h:128 * h + 128], attn[q][b][:, :], qq == 0, False,
                               [('vtok', tt), ('attn', q, b)], [obk], sgc=True)
                            MM(ob[:, qq * 128:qq * 128 + 64], Sall[q][:, 2 * tt, :], gq[q][0:64, tt * 128:tt * 128 + 64], False, False,
                               [('Sall', q, 2 * tt), ('gq', q, tg)], [obk], sgc=True)
                            MM(ob[:, qq * 128 + 64:qq * 128 + 128], Sall[q][:, 2 * tt + 1, :], gq[q][0:64, tt * 128 + 64:tt * 128 + 128], False, True,
                               [('Sall', q, 2 * tt + 1), ('gq', q, tg)], [obk], sgc=True)
                        g_run()
                    for q in range(2):
                        def mkhn(q=q, tg=tg, gsl=gsl):
                            h = 2 * pp + q
                            obn = q if tg % 2 == 0 else 4 + q
                            ob = ps[obn]
                            obk = ('ps', obn)

                            def h1():
                                CP('dve', of[q][:, :], ob[:, :], [obk], [('of', q)])
                                ACT(osq[q][:, :], of[q][:, :], AF.Square, [('of', q)], [('osq', q)])

                            def h2():
                                MM(ps[2 + q][:, :], onesb[:, :], osq[q][:, :], True, True, [('osq', q)], [('ps', 2 + q)])

                            def h3():
                                ACT(rst[q][:, :], ps[2 + q][:, :], AF.Ln, [('ps', 2 + q)], [('rst', q)], scale=1.0 / 128, bias=EPS)
                                ACT(rst[q][:, :], rst[q][:, :], AF.Exp, [('rst', q)], [('rst', q)], scale=-0.5)

                            def h4():
                                STT('dve', of[q][:, :], of[q][:, :], ggl[:, h:h + 1], rst[q][:, :], ALU.mult, ALU.mult, [('of', q), ('rst', q)], [('of', q)])
                                TT('dve', ogT[:, h, gsl], of[q][:, :], sog[q][:, gsl], ALU.mult, [('of', q), ('sog', q, tg)], [('ogT', h, tg)])
                            h1()
                            gdef.append([1, h2])
                            gdef.append([2, h3])
                            gdef.append([3, h4])
                        mkhn()
                while gdef:
                    g_run()
            S.flush()
            if 'ogT' in dbg_out:
                DMA('pool', dbg_out['ogT'], ogT[:, :, :], [], ())
                S.flush()
        if stop_after == 'GLA':
            sB.close(); sA.close()
            break

        with ExitStack() as st:
            wq = sb(st, "wsq", [128, 8, 512], BF)
            wk = sb(st, "wsk", [128, 8, 512], BF)
            wv = sb(st, "wsv", [128, 8, 512], BF)
            DMA('pool', wv[:, :, :], w_in_v[:, :, 1024:1536], (), ['wsv'])
            DMA('pool', wq[:, :, :], w_in_v[:, :, 0:512], (), ['wsq'])
            DMA('pool', wk[:, :, :], w_in_v[:, :, 512:1024], (), ['wsk'])
            vtok = sb(st, "vts", [128, 16, 512], BF)
            sqp = [sb(st, f"sq{i}", [128, SEQ], BF) for i in range(2)]
            skp = [[sb(st, f"sk{i}{j}", [128, SEQ], BF) for j in range(2)] for i in range(2)]
            for i_ in range(2):
                MS('pool', skp[i_][0][64:128, :], 0.0, [('skz', i_, 0)])
                MS('pool', skp[i_][1][0:64, :], 0.0, [('skz', i_, 1)])
            esb = [sb(st, f"esb{i}", [128, 512], BF) for i in range(3)]
            spb = [sb(st, f"spb{i}", [128, 512], BF) for i in range(2)]
            wsb = [sb(st, f"wsb{i}", [128, 512], BF) for i in range(2)]
            Ls = [sb(st, f"Ls{i}", [128, 512], BF) for i in range(2)]
            osq = sb(st, "osq", [128, 512], BF)
            of = sb(st, "of", [128, 512], F32)
            rst = sb(st, "rst", [128, 512], F32)
            MS('dve', osq[:, :], 0.0, ['osq'])
            def v_chunk(tt):
                bank = 2 + (pj[0] % 2)
                pj[0] += 1
                tsl = slice(tt * 128, (tt + 1) * 128)
                for kt in range(8):
                    MM(ps[bank][:, :], hT[:, kt, tsl], wv[:, kt, :], kt == 0, kt == 7, [('hT', tt), 'wsv'], [('ps', bank)])
                CP('dve', vtok[:, tt, :], ps[bank][:, :], [('ps', bank)], [('vts', tt)])
            for tt in range(4):
                v_chunk(tt)
            tiles = [(h, G, kb) for h in range(8) for G in range(4) for kb in range(4 * G + 3, -1, -1)]
            NT = len(tiles)
            spb3 = spb + [sb(st, "spb2", [128, 512], BF)]
            wsb3 = wsb + [sb(st, "wsb2", [128, 512], BF)]

            pending = []

            def sb_proj_chunks(P):
                pp = P % 2
                out = []
                for which in ('q', 'k'):
                    W = wq if which == 'q' else wk
                    for tg in range(4):
                        def mk(W=W, which=which, tg=tg):
                            state = {}
                            tsl_ = slice(tg * 512, (tg + 1) * 512)

                            def c1():
                                bank = 2 + (pj[0] % 2)
                                pj[0] += 1
                                state['bank'] = bank
                                for kt in range(4):
                                    MM(ps[bank][:, :], W[:, kt, 128 * P:128 * P + 128], hT[:, kt, tsl_],
                                       kt == 0, False, ['wsq' if which == 'q' else 'wsk'], [('ps', bank)])

                            def c2():
                                bank = state['bank']
                                for kt in range(4, 8):
                                    MM(ps[bank][:, :], W[:, kt, 128 * P:128 * P + 128], hT[:, kt, tsl_],
                                       False, kt == 7, ['wsq' if which == 'q' else 'wsk'], [('ps', bank)])
                                if which == 'q':
                                    TS('dve', sqp[pp][:, tsl_], ps[bank][:, :], 0.125, None, ALU.mult, None,
                                       [('ps', bank)], [('sq', pp, tg)])
                                else:
                                    CP('dve', skp[pp][0][0:64, tsl_], ps[bank][0:64, :], [('ps', bank)], [('sk', pp, 0, tg)])
                                    CP('dve', skp[pp][1][64:128, tsl_], ps[bank][64:128, :], [('ps', bank)], [('sk', pp, 1, tg)])
                            return [c1, c2]
                        out.extend(mk())
                return out

            def tinfo(i):
                h, G, kb = tiles[i]
                j = kb - 4 * G
                c0 = 128 * j if j > 0 else 0
                pp = (h // 2) % 2
                return dict(h=h, pp=pp, G=G, kb=kb, c0=c0, diag=(j >= 0), first=(kb == 4 * G + 3), last=(kb == 0),
                            t0=512 * G + c0, t1=512 * G + 512, ksl=slice(kb * 128, (kb + 1) * 128),
                            sq=sqp[pp], sk=skp[pp][h % 2],
                            kkey=('sk', pp, h % 2, kb // 4), qkey=('sq', pp, G), zkey=('skz', pp, h % 2),
                            lk=('L', G % 2), L=Ls[G % 2],
                            obk=('ps', 4 + G % 2), ob=ps[4 + G % 2])

            ZB = (0, 1, 6)

            def stA(i):
                t = tinfo(i)
                c0 = t['c0']
                if t['first']:
                    if t['G'] == 0 and t['h'] % 2 == 0:
                        while pending:
                            pending.pop(0)()
                        if t['h'] // 2 + 1 < 4:
                            pending.extend(sb_proj_chunks(t['h'] // 2 + 1))
                    MS('pool', t['L'][:, :], 0.0, [t['lk']])
                elif pending and not t['diag']:
                    pending.pop(0)()
                zbn = ZB[i % 3]
                zb = ps[zbn]
                MM(zb[:, c0:512], t['sk'][:, t['ksl']], t['sq'][:, t['t0']:t['t1']], True, False,
                   [t['kkey'], t['qkey'], t['zkey']], [('ps', zbn)], sgc=True)
                ACT(esb[i % 3][:, c0:512], zb[:, c0:512], AF.Exp, [('ps', zbn)], [('esb', i % 3)])
                ACT(spb3[i % 3][:, c0:512], esb[i % 3][:, c0:512], AF.Ln, [('esb', i % 3)], [('spb', i % 3)], bias=1.0)
                if t['diag']:
                    TT('dve', spb3[i % 3][:, c0:c0 + 128], spb3[i % 3][:, c0:c0 + 128], mstrict[:, :], ALU.mult,
                       [('spb', i % 3)], [('spb', i % 3)])

            def stB(i):
                t = tinfo(i)
                c0 = t['c0']
                zbn = ZB[i % 3]
                ab = ps[zbn]
                abk = ('ps', zbn)
                MM(ab[:, c0:512], triinc[:, :], spb3[i % 3][:, c0:512], False, t['first'], [('spb', i % 3)], [abk], sgc=True)
                if not t['first']:
                    MM(ab[:, c0:512], onesneg[:, :], t['L'][:, c0:512], False, True, [t['lk']], [abk], sgc=True)
                ACT(wsb3[i % 3][:, c0:512], ab[:, c0:512], AF.Exp, [abk], [('wsb', i % 3)])
                if t['diag']:
                    TT('dve', wsb3[i % 3][:, c0:c0 + 128], wsb3[i % 3][:, c0:c0 + 128], mstrict[:, :], ALU.mult,
                       [('wsb', i % 3)], [('wsb', i % 3)])
                if t['kb'] > 0:
                    TT('dve', t['L'][:, c0:512], t['L'][:, c0:512], spb3[i % 3][:, c0:512], ALU.add,
                       [t['lk'], ('spb', i % 3)], [t['lk']])

            def stC(i):
                t = tinfo(i)
                c0 = t['c0']
                h, G = t['h'], t['G']
                ob, obk = t['ob'], t['obk']
                pr = h // 2
                MM(ob[:, c0:512], vtok[:, t['kb'], 128 * pr:128 * pr + 128], wsb3[i % 3][:, c0:512], t['first'], t['last'],
                   [('vts', t['kb']), ('wsb', i % 3)], [obk], sgc=True)
                if t['last']:
                    gsl = slice(G * 512, (G + 1) * 512)
                    hp = 64 * (h % 2)
                    rows = slice(hp, hp + 64)

                    def hn1():
                        CP('dve', of[rows, :], ob[rows, :], [obk], [('of', hp)])
                        TT('dve', osq[rows, :], of[rows, :], of[rows, :], ALU.mult, [('of', hp)], ['osq'])

                    def hn2():
                        MM(ps[7][:, :], bdones[:, :], osq[:, :], True, True, ['osq'], [('ps', 7)])

                    def hn3():
                        ACT(rst[rows, :], ps[7][rows, :], AF.Ln, [('ps', 7)], [('rst', hp)], scale=1.0 / 64, bias=EPS)
                        ACT(rst[rows, :], rst[rows, :], AF.Exp, [('rst', hp)], [('rst', hp)], scale=-0.5)

                    def hn4():
                        STT('dve', osT[rows, pr, gsl], of[rows, :], gsb[rows, pr:pr + 1], rst[rows, :], ALU.mult, ALU.mult,
                            [('of', hp), ('rst', hp)], [('osT', h, G)])
                    deferred.append([1, hn1])
                    deferred.append([2, hn2])
                    deferred.append([3, hn3])
                    deferred.append([4, hn4])

            deferred = []

            def run_deferred(all_=False):
                for d_ in list(deferred):
                    d_[0] -= 1
                for d_ in list(deferred):
                    if d_[0] <= 0 or all_:
                        d_[1]()
                        deferred.remove(d_)

            for c_ in sb_proj_chunks(0):
                c_()
            for k in range(NT + 2):
                if 4 + k < NTT:
                    v_chunk(4 + k)
                if k < NT:
                    stA(k)
                if 1 <= k <= NT:
                    stB(k - 1)
                run_deferred()
                if k >= 2:
                    stC(k - 2)
            while deferred:
                run_deferred()
            S.flush()
            if 'osT' in dbg_out:
                DMA('pool', dbg_out['osT'], osT[:, :, :], [], ())
                S.flush()
        sB.close()
        if stop_after == 'SB':
            sA.close()
            break

        sR = ExitStack()
        x1 = sb(sR, "x1", [128, NTT, D], F32, side='right')
        h2T = sb(sR, "h2T", [128, 8, 1024], BF, side='right')
        xn = [sb(sR, f"xn{i}", [128, D], BF, side='right') for i in range(2)]
        junk = sb(sR, "junk", [128, D], BF, side='right')

        def n2_front(hf_, i):
            c = hf_ * 8 + i
            tt_ = hf_ * 8 + i
            b_ = i % 2
            src = x1[:, tt_, :]
            skeys = [('x1', tt_)]
            ACT(junk[:, :], src, AF.Square, skeys, ['junk', ('ss', c)], accum_out=ss[:, c:c + 1])
            ACT(rs[:, c:c + 1], ss[:, c:c + 1], AF.Ln, [('ss', c)], [('rs', c)], scale=1.0 / D, bias=EPS)
            ACT(rs[:, c:c + 1], rs[:, c:c + 1], AF.Exp, [('rs', c)], [('rs', c)], scale=-0.5)
            TS('dve', xn[b_][:, :], src, rs[:, c:c + 1], None, ALU.mult, None, skeys + [('rs', c)], [('xn', b_)])

        def n2_back(hf_, i):
            b_ = i % 2
            for kt in range(8):
                TR(psb[b_][:, kt * 128:(kt + 1) * 128], xn[b_][:, kt * 128:(kt + 1) * 128], ident[:, :],
                   [('xn', b_)], [('ps', b_)])
            TT('dve', h2T[:, :, i * 128:(i + 1) * 128],
               psb[b_].rearrange("p (k t) -> p k t", k=8),
               gb2[:, :].rearrange("p (k t) -> p k t", k=8), ALU.mult,
               [('ps', b_)], [('h2T', i)])


        with ExitStack() as st:
            wos = sb(st, "wos", [128, 4, D], BF)
            wogl = sb(st, "wogl", [128, 4, D], BF)
            DMA('pool', wos[:, :, :], w_out_sb_v, (), ['wos'])
            DMA('pool', wogl[:, :, :], w_out_gl_v, (), ['wogl'])
            xt = [sb(st, f"xt{i}", [128, D], F32) for i in range(2)]
            cnt = 0
            MS('dve', ss[:, 0:16], 0.0, SS16)
            for tt in range(NTT):
                b = tt % 2
                tsl = slice(tt * 128, (tt + 1) * 128)
                DMA('sp', xt[b][:, :], x_d[seq, tsl, :], (), [('xt', b)])
                for cg in range(2):
                    bank = 2 + cnt % 4
                    cnt += 1
                    csl = slice(cg * 512, (cg + 1) * 512)
                    for h in range(4):
                        MM(ps[bank][:, :], osT[:, h, tsl], wos[:, h, csl], h == 0, False, ['wos'], [('ps', bank)])
                    for h in range(4):
                        MM(ps[bank][:, :], ogT[:, h, tsl], wogl[:, h, csl], False, h == 3, ['wogl'], [('ps', bank)])
                    TT('dve', x1[:, tt, csl], ps[bank][:, :], xt[b][:, csl], ALU.add, [('ps', bank), ('xt', b)], [('x1', tt)])
                if tt < 8:
                    n2_front(0, tt)
                if 1 <= tt <= 8:
                    n2_back(0, tt - 1)
            if 'x1' in dbg_out:
                dump('x1', x1[:, :, :], [('x1', i) for i in range(NTT)])
            S.flush()
        sA.close()
        if stop_after == 'P3':
            sR.close()
            break

        with ExitStack() as st:
            gT = sb(st, "gT", [128, 22, 1024], BF)
            wd = sb(st, "wd", [128, 22, 512], BF)
            ARN = 3 * 2048 + 3 * 1028 + 4 * 1024 + 1024
            arena = sb(st, "arena", [128, ARN], BF)
            off = 0
            wup = []
            for i_ in range(3):
                wup.append(arena[:, off:off + 2048].rearrange("p (k c) -> p k c", k=8))
                off += 2048
            ybuf = []
            for i_ in range(3):
                ybuf.append(arena[:, off:off + 1028].bitcast(F32))
                off += 1028
            acc = []
            for i_ in range(4):
                acc.append(arena[:, off:off + 1024].bitcast(F32))
                off += 1024
            sa = arena[:, off:off + 1024].bitcast(F32)
            wd2 = arena[:, 0:22 * 512].rearrange("p (j c) -> p j c", j=22)
            ALIAS = ([('wup', i_, p_) for i_ in range(3) for p_ in range(2)] + [('yb', i_) for i_ in range(3)]
                     + [('acc', i_) for i_ in range(4)] + ['sa'])
            xo = [sb(st, f"xo{i}", [128, D], F32) for i in range(2)]
            ssF = sb(st, "ssF", [128, 16], F32)
            rsF = sb(st, "rsF", [128, 16], F32)
            if seq == 0:
                print("P5 sbuf bytes remaining", nc.sbuf_bytes_remaining, flush=True)

            MS('dve', ssF[:, 0:16], 0.0, [('ssF', i_) for i_ in range(16)])
            cnt = 0
            yi = 0
            ai = 0
            for hf in range(2):
                def ld_wup(jj):
                    wb_ = jj % 3
                    DMA('pool', wup[wb_][:, :, 0:128], w_up_v[:, :, 128 * jj:128 * jj + 128], (), [('wup', wb_, 0)])
                    DMA('pool', wup[wb_][:, :, 128:256], w_up_v[:, :, DFF + 128 * jj:DFF + 128 * jj + 128], (), [('wup', wb_, 1)])
                ld_wup(0)
                ld_wup(1)
                for j in range(22):
                    wb = j % 3
                    if j + 2 < 22:
                        ld_wup(j + 2)
                    if j == 2:
                        DMA('pool', wd[:, :, :], w_dn_v[:, :, 0:512], (), ['wd'])
                    for g2 in range(2):
                        first_group = (hf == 0 and g2 == 0)
                        last_group = (hf == 1 and g2 == 1)
                        accs = []
                        for part in range(2):
                            ct = part * 22 + j
                            bank = 2 + cnt % 4
                            cnt += 1
                            for kt in range(8):
                                MM(ps[bank][:, :], wup[wb][:, kt, part * 128:(part + 1) * 128],
                                   h2T[:, kt, g2 * 512:(g2 + 1) * 512], kt == 0, kt == 7,
                                   [('wup', wb, part)] + hkeys('h2T', g2 * 512, g2 * 512 + 512), [('ps', bank)])
                            yb = ybuf[yi % 3]
                            ykey = ('yb', yi % 3)
                            yi += 1
                            ACT(yb[:, 2:514], ps[bank][:, :], AF.Copy, [('ps', bank)], [ykey])
                            if first_group:
                                MS('pool', yb[:, 0:2], 0.0, [ykey])
                            else:
                                CP('pool', yb[:, 0:2], carry[:, ct, :], [('carry', ct)], [ykey])
                            if not last_group:
                                CP('pool', carry[:, ct, :], yb[:, 512:514], [ykey], [('carry', ct)])
                            a = acc[ai % 4]
                            akey = ('acc', ai % 4)
                            ai += 1
                            ACT(a[:, :], ps[bank][:, :], AF.Identity, [('ps', bank)], [akey],
                                scale=cw[:, 88 + ct:88 + ct + 1], bias=cb[:, ct:ct + 1])
                            STT('dve', a[:, :], yb[:, 1:513], cw[:, 44 + ct:44 + ct + 1], a[:, :], ALU.mult, ALU.add,
                                [ykey, akey], [akey])
                            STT('dve', a[:, :], yb[:, 0:512], cw[:, ct:ct + 1], a[:, :], ALU.mult, ALU.add,
                                [ykey, akey], [akey])
                            accs.append((a, akey))
                        ACT(sa[:, :], accs[0][0][:, :], AF.Silu, [accs[0][1]], ['sa'])
                        TT('dve', gT[:, j, g2 * 512:(g2 + 1) * 512], sa[:, :], accs[1][0][:, :], ALU.mult,
                           ['sa', accs[1][1]], [('gT', j, g2)])
                DMA('pool', wd2[:, :, :], w_dn_v[:, :, 512:1024], (), ['wd2'] + ALIAS)
                for cg in range(2):
                    csl = slice(cg * 512, (cg + 1) * 512)
                    wdt = wd if cg == 0 else wd2
                    wkeys = ['wd'] if cg == 0 else (['wd2'] + ALIAS)
                    for i in range(8):
                        tt = hf * 8 + i
                        bank = 6 + i % 2
                        if hf == 0 and cg == 0:
                            n2_front(1, i)
                        for j in range(22):
                            MM(ps[bank][:, :], gT[:, j, i * 128:(i + 1) * 128], wdt[:, j, :], j == 0, j == 21,
                               [('gT', j, i // 4)] + wkeys, [('ps', bank)])
                        TT('dve', x1[:, tt, csl], ps[bank][:, :], x1[:, tt, csl], ALU.add, [('ps', bank), ('x1', tt)], [('x1', tt)])
                        if hf == 0 and cg == 0:
                            n2_back(1, i)
                        if cg == 1:
                            c = hf * 8 + i
                            b = i % 2
                            tsl = slice(tt * 128, (tt + 1) * 128)
                            ACT(junk[:, :], x1[:, tt, :], AF.Square, [('x1', tt)], ['junk', ('ssF', c)], accum_out=ssF[:, c:c + 1])
                            ACT(rsF[:, c:c + 1], ssF[:, c:c + 1], AF.Ln, [('ssF', c)], [('rsF', c)], scale=1.0 / D, bias=EPS)
                            ACT(rsF[:, c:c + 1], rsF[:, c:c + 1], AF.Exp, [('rsF', c)], [('rsF', c)], scale=-0.5)
                            STT('dve', xo[b][:, :], x1[:, tt, :], rsF[:, c:c + 1], fing[:, :], ALU.mult, ALU.mult,
                                [('x1', tt), ('rsF', c)], [('xo', b)])
                            DMA('sp', out_d[seq, tsl, :], xo[b][:, :], [('xo', b)], ())
            S.flush()
        sR.close()

    top.close()
    return nc


_NC_CACHE = {}


def _prep_inputs(inputs):
    f = lambda a: np.ascontiguousarray(np.asarray(a, dtype=np.float32))
    g1 = f(inputs["attn_norm_g"])[0]
    g2 = f(inputs["ffn_norm_g"])[0]
    shared = {
        "w_in": f(inputs["w_in"])[0],
        "w_out": f(inputs["w_out"])[0],
        "w_up": f(inputs["w_ffn_up"])[0],
        "w_dn": f(inputs["w_ffn_down"])[0],
        "wgu": f(inputs["w_gate_up"])[0],
        "bgu": f(inputs["b_gate_up"])[0].reshape(1, 256),
        "gb1": np.ascontiguousarray(np.repeat(g1.reshape(8, 128).T[:, :, None], 128, axis=2).reshape(128, D)),
        "gb2": np.ascontiguousarray(np.repeat(g2.reshape(8, 128).T[:, :, None], 128, axis=2).reshape(128, D)),
        "fing": np.ascontiguousarray(np.broadcast_to(f(inputs["final_norm_g"])[None, :], (128, D))),
        "gsb": np.ascontiguousarray(f(inputs["sb_out_g"])[0].reshape(4, 128).T),
        "ggl": np.ascontiguousarray(f(inputs["gla_out_g"])[0].reshape(4, 128).T),
        "cw": np.ascontiguousarray(f(inputs["conv_w"])[0].reshape(3, 44, 128).transpose(2, 0, 1).reshape(128, 132)),
        "cb": np.ascontiguousarray(f(inputs["conv_b"])[0].reshape(44, 128).T),
    }
    for k, v in _consts().items():
        shared["c_" + k] = v
    return shared


def kernel(**inputs):
    x = np.ascontiguousarray(np.asarray(inputs["x"], dtype=np.float32))
    B = x.shape[0]
    nseq = B // NCORES
    shared = _prep_inputs(inputs)
    if nseq not in _NC_CACHE:
        _NC_CACHE[nseq] = build(nseq)
    nc = _NC_CACHE[nseq]
    in_maps = []
    for c in range(NCORES):
        m = dict(shared)
        m["x"] = x[c * nseq:(c + 1) * nseq]
        in_maps.append(m)
    res = run_bass_kernel_spmd(nc, in_maps, core_ids=list(range(NCORES)))
    return np.concatenate([r["out"] for r in res.results], axis=0)
```
